# Optimizing a Trainium2 kernel written in Bass

```python
import functools
import jax, jax.numpy as jnp
from jax import lax
import numpy as np

D_MODEL = 4096
BATCH = 8
SEQ = 2048
DEPTH = 2
DEC_BATCH = 8
DEC_SEQ = 64
PAST_LEN = 1024

CHUNK = 64
HEAD_DIM = 128
W_MIX = D_MODEL
W_A = W_MIX // 4
W_B = W_MIX // 4
W_C = W_MIX // 4
W_D = W_MIX - W_A - W_B - W_C
H_B = W_B // HEAD_DIM
H_C = W_C // HEAD_DIM
ROPE_DIM = HEAD_DIM // 4
ROPE_THETA = 500000.0
IDX_HEADS = 16
IDX_DIM = 64
IDX_ROPE_DIM = IDX_DIM // 4
TOPK_MAX = 256
CONV_A_W = 3
CONV_D_W = 4
LRU_BLOCKS = 8
LRU_BW = W_D // LRU_BLOCKS
LRU_C = 8.0
D_FF = 2 * D_MODEL
IN_SIZES = (W_A, W_A, W_A, W_B, W_B, W_B, IDX_HEADS * IDX_DIM, IDX_DIM, IDX_HEADS, W_C, W_C, W_C, H_C, W_D, W_D)
N_IN = sum(IN_SIZES)
ALPHA = (2.0 * DEPTH) ** 0.25
BETA = (8.0 * DEPTH) ** -0.25
QBLK_FOX = 128
QBLK_DSA = CHUNK
LN_EPS = 1e-5

kernel_name = "hybrid_streaming_encoder_step"


def split_cols(p, sizes):
    out, start = [], 0
    for s in sizes:
        out.append(p[..., start:start + s])
        start += s
    return out


def layer_norm(x, g, b):
    xf = x.astype(jnp.float32)
    mu = xf.mean(-1, keepdims=True)
    var = jnp.square(xf - mu).mean(-1, keepdims=True)
    return ((xf - mu) * lax.rsqrt(var + LN_EPS) * g + b).astype(x.dtype)


def swiglu(x, w13, w2):
    g, u = jnp.split(x @ w13, 2, axis=-1)
    return (jax.nn.silu(g) * u) @ w2


def partial_rope(x, pos, rot_dim):
    half = rot_dim // 2
    inv = ROPE_THETA ** (-jnp.arange(half, dtype=jnp.float32) / half)
    ang = pos.astype(jnp.float32)[:, None] * inv[None, :]
    ang = ang.reshape((ang.shape[0],) + (1,) * (x.ndim - 3) + (half,))
    cos, sin = jnp.cos(ang), jnp.sin(ang)
    xf = x[..., :rot_dim].astype(jnp.float32)
    x1, x2 = xf[..., :half], xf[..., half:]
    rot = jnp.concatenate([x1 * cos - x2 * sin, x2 * cos + x1 * sin], axis=-1).astype(x.dtype)
    return jnp.concatenate([rot, x[..., rot_dim:]], axis=-1)


def causal_dwconv(x, state, w, b=None):
    width = w.shape[0]
    T = x.shape[1]
    xp = jnp.concatenate([state.astype(x.dtype), x], axis=1)
    out = xp[:, 0:T] * w[0]
    for j in range(1, width):
        out = out + xp[:, j:j + T] * w[j]
    if b is not None:
        out = out + b
    return out, xp[:, T:]


def rg_lru(x, h0, wa, ba, wx, bx, lam):
    B, T, C = x.shape
    xb = x.reshape(B, T, LRU_BLOCKS, LRU_BW)
    r = jax.nn.sigmoid((jnp.einsum('btnc,ncd->btnd', xb, wa).reshape(B, T, C) + ba).astype(jnp.float32))
    i = jax.nn.sigmoid((jnp.einsum('btnc,ncd->btnd', xb, wx).reshape(B, T, C) + bx).astype(jnp.float32))
    log_a = -LRU_C * r * jax.nn.softplus(-lam.astype(jnp.float32))
    a = jnp.exp(log_a)
    u = jnp.sqrt(-jnp.expm1(2.0 * log_a)) * i * x.astype(jnp.float32)
    u = u.at[:, 0].add(a[:, 0] * h0.astype(jnp.float32))

    def combine(e1, e2):
        a1, b1 = e1
        a2, b2 = e2
        return a1 * a2, a2 * b1 + b2

    _, hs = lax.associative_scan(combine, (a, u), axis=1)
    return hs.astype(x.dtype), hs[:, -1].astype(x.dtype)


def dsa_block(q, qi, wi, q_pos, k, v, ki, k_pos, n_sel):
    s_idx = jnp.einsum('bqhd,bsd->bqhs', qi, ki).astype(jnp.float32) * IDX_DIM ** -0.5
    score = jnp.einsum('bqhs,bqh->bqs', jax.nn.relu(s_idx), wi.astype(jnp.float32))
    adm = (k_pos // CHUNK)[None, :] <= (q_pos // CHUNK)[:, None]
    score = jnp.where(adm[None], score, -jnp.inf)
    top_val, top_idx = lax.top_k(score, n_sel)
    valid = jnp.isfinite(top_val)
    gather = jax.vmap(lambda rows, ids: rows[ids])
    kg = gather(k, top_idx)
    vg = gather(v, top_idx)
    logits = jnp.einsum('bqhd,bqkhd->bqhk', q, kg).astype(jnp.float32) * HEAD_DIM ** -0.5
    logits = jnp.where(valid[:, :, None, :], logits, -jnp.inf)
    p = jax.nn.softmax(logits, axis=-1).astype(v.dtype)
    return jnp.einsum('bqhk,bqkhd->bqhd', p, vg)


def fox_block(q, fq, q_pos, k, v, fk, k_pos):
    logits = jnp.einsum('bqhd,bshd->bhqs', q, k).astype(jnp.float32) * HEAD_DIM ** -0.5
    logits = logits + jnp.transpose(fq, (0, 2, 1))[..., :, None] - jnp.transpose(fk, (0, 2, 1))[..., None, :]
    causal = k_pos[None, :] <= q_pos[:, None]
    logits = jnp.where(causal[None, None], logits, -jnp.inf)
    p = jax.nn.softmax(logits, axis=-1).astype(v.dtype)
    return jnp.einsum('bhqs,bshd->bqhd', p, v)


def sweep_queries(fn, blk, q_args, q_pos, kv_args):
    T = q_pos.shape[0]
    blk = min(blk, T)
    nb = T // blk

    def to_blocks(a):
        return jnp.moveaxis(a.reshape((a.shape[0], nb, blk) + a.shape[2:]), 1, 0)

    qb = tuple(to_blocks(a) for a in q_args)
    pb = q_pos.reshape(nb, blk)
    out = lax.map(lambda args: fn(*args[0], args[1], *kv_args), (qb, pb))
    out = jnp.moveaxis(out, 0, 1)
    return out.reshape((out.shape[0], T) + out.shape[3:])


def token_mix(h, past, w_in, b_f, conv_a_w, conv_d_w, conv_d_b, lru_wa, lru_ba, lru_wx, lru_bx, lru_lam, w_out):
    pk_b, pv_b, pki_b, pk_c, pv_c, plf_c, st_a, st_d, h0 = past
    B, T, _ = h.shape
    pos0 = pk_b.shape[1]
    (a_h, a_b, a_c, bq, bk, bv, bqi, bki, bwi, cq, ck, cv, cf, dx, dg) = split_cols(h @ w_in, IN_SIZES)
    q_pos = pos0 + jnp.arange(T, dtype=jnp.int32)

    conv_out, new_st_a = causal_dwconv(a_c * a_h, st_a, conv_a_w)
    y_a = a_b * conv_out

    bq = partial_rope(bq.reshape(B, T, H_B, HEAD_DIM), q_pos, ROPE_DIM)
    bk = partial_rope(bk.reshape(B, T, H_B, HEAD_DIM), q_pos, ROPE_DIM)
    bv = bv.reshape(B, T, H_B, HEAD_DIM)
    bqi = partial_rope(bqi.reshape(B, T, IDX_HEADS, IDX_DIM), q_pos, IDX_ROPE_DIM)
    bki = partial_rope(bki, q_pos, IDX_ROPE_DIM)
    kb_all = jnp.concatenate([pk_b.astype(h.dtype), bk], axis=1)
    vb_all = jnp.concatenate([pv_b.astype(h.dtype), bv], axis=1)
    kib_all = jnp.concatenate([pki_b.astype(h.dtype), bki], axis=1)
    S = pos0 + T
    k_pos = jnp.arange(S, dtype=jnp.int32)
    n_sel = min(TOPK_MAX, S // 4)
    y_b = sweep_queries(functools.partial(dsa_block, n_sel=n_sel), QBLK_DSA,
                        (bq, bqi, bwi * IDX_HEADS ** -0.5), q_pos, (kb_all, vb_all, kib_all, k_pos))

    logf = jax.nn.log_sigmoid((cf + b_f).astype(jnp.float32))
    F = jnp.cumsum(jnp.concatenate([plf_c.astype(jnp.float32), logf], axis=1), axis=1)
    cq = cq.reshape(B, T, H_C, HEAD_DIM)
    ck = ck.reshape(B, T, H_C, HEAD_DIM)
    cv = cv.reshape(B, T, H_C, HEAD_DIM)
    kc_all = jnp.concatenate([pk_c.astype(h.dtype), ck], axis=1)
    vc_all = jnp.concatenate([pv_c.astype(h.dtype), cv], axis=1)
    y_c = sweep_queries(fox_block, QBLK_FOX, (cq, F[:, pos0:]), q_pos, (kc_all, vc_all, F, k_pos))

    xc, new_st_d = causal_dwconv(dx, st_d, conv_d_w, conv_d_b)
    h_lru, h_last = rg_lru(xc, h0, lru_wa, lru_ba, lru_wx, lru_bx, lru_lam)
    y_d = jax.nn.gelu(dg) * h_lru

    y = jnp.concatenate([y_a, y_b.reshape(B, T, W_B), y_c.reshape(B, T, W_C), y_d], axis=-1) @ w_out
    new_state = (bk, bv, bki, ck, cv, logf.astype(h.dtype), new_st_a, new_st_d, h_last)
    return y, new_state


def trunk(x, past, ln_g, ln_b, ffn_w13, ffn_w2, w_in, fox_b_f, conv_a_w, conv_d_w, conv_d_b,
          lru_wa, lru_ba, lru_wx, lru_bx, lru_lambda, w_out):
    layer_states = []
    for l in range(DEPTH):
        x = layer_norm(ALPHA * x + 0.5 * swiglu(x, ffn_w13[l, 0], ffn_w2[l, 0]), ln_g[l, 0], ln_b[l, 0])
        y, st = token_mix(x, tuple(c[l] for c in past), w_in[l], fox_b_f[l], conv_a_w[l], conv_d_w[l],
                          conv_d_b[l], lru_wa[l], lru_ba[l], lru_wx[l], lru_bx[l], lru_lambda[l], w_out[l])
        x = layer_norm(ALPHA * x + y, ln_g[l, 1], ln_b[l, 1])
        x = layer_norm(ALPHA * x + 0.5 * swiglu(x, ffn_w13[l, 1], ffn_w2[l, 1]), ln_g[l, 2], ln_b[l, 2])
        layer_states.append(st)
    new_state = tuple(jnp.stack([st[i] for st in layer_states]) for i in range(len(layer_states[0])))
    return x, new_state


def setup_inputs(seed: int = 0) -> dict:
    key = jax.random.key(seed)
    ks = jax.random.split(key, 32)

    def nrm(k, shape, scale):
        return jax.random.normal(k, shape, jnp.float32) * scale

    u = jax.random.uniform(ks[24], (DEPTH, W_D), jnp.float32, 0.9, 0.999)
    a = u ** (1.0 / LRU_C)
    return {
        "x_prompt": nrm(ks[0], (BATCH, SEQ, D_MODEL), 1.0),
        "x_sample": nrm(ks[1], (DEC_BATCH, DEC_SEQ, D_MODEL), 1.0),
        "cache_dsa_k": nrm(ks[2], (DEPTH, DEC_BATCH, PAST_LEN, H_B, HEAD_DIM), 1.0),
        "cache_dsa_v": nrm(ks[3], (DEPTH, DEC_BATCH, PAST_LEN, H_B, HEAD_DIM), 1.0),
        "cache_dsa_kidx": nrm(ks[4], (DEPTH, DEC_BATCH, PAST_LEN, IDX_DIM), 1.0),
        "cache_fox_k": nrm(ks[5], (DEPTH, DEC_BATCH, PAST_LEN, H_C, HEAD_DIM), 1.0),
        "cache_fox_v": nrm(ks[6], (DEPTH, DEC_BATCH, PAST_LEN, H_C, HEAD_DIM), 1.0),
        "cache_fox_logf": jax.nn.log_sigmoid(nrm(ks[7], (DEPTH, DEC_BATCH, PAST_LEN, H_C), 1.0) + 3.0),
        "state_conv_a": nrm(ks[8], (DEPTH, DEC_BATCH, CONV_A_W - 1, W_A), 1.0),
        "state_conv_d": nrm(ks[9], (DEPTH, DEC_BATCH, CONV_D_W - 1, W_D), 1.0),
        "state_lru": nrm(ks[10], (DEPTH, DEC_BATCH, W_D), 0.5),
        "ln_g": 1.0 + nrm(ks[11], (DEPTH, 3, D_MODEL), 0.02),
        "ln_b": nrm(ks[12], (DEPTH, 3, D_MODEL), 0.02),
        "ffn_w13": nrm(ks[13], (DEPTH, 2, D_MODEL, 2 * D_FF), D_MODEL ** -0.5),
        "ffn_w2": nrm(ks[14], (DEPTH, 2, D_FF, D_MODEL), BETA * D_FF ** -0.5),
        "w_in": nrm(ks[15], (DEPTH, D_MODEL, N_IN), D_MODEL ** -0.5),
        "fox_b_f": jax.random.uniform(ks[16], (DEPTH, H_C), jnp.float32, 1.0, 5.0),
        "conv_a_w": nrm(ks[17], (DEPTH, CONV_A_W, W_A), CONV_A_W ** -0.5),
        "conv_d_w": nrm(ks[18], (DEPTH, CONV_D_W, W_D), CONV_D_W ** -0.5),
        "conv_d_b": nrm(ks[19], (DEPTH, W_D), 0.02),
        "lru_wa": nrm(ks[20], (DEPTH, LRU_BLOCKS, LRU_BW, LRU_BW), LRU_BW ** -0.5),
        "lru_ba": nrm(ks[21], (DEPTH, W_D), 0.02),
        "lru_wx": nrm(ks[22], (DEPTH, LRU_BLOCKS, LRU_BW, LRU_BW), LRU_BW ** -0.5),
        "lru_bx": nrm(ks[23], (DEPTH, W_D), 0.02),
        "lru_lambda": jnp.log(a) - jnp.log1p(-a),
        "w_out": nrm(ks[25], (DEPTH, W_MIX, D_MODEL), BETA * W_MIX ** -0.5),
    }


def reference(x_prompt, x_sample, cache_dsa_k, cache_dsa_v, cache_dsa_kidx, cache_fox_k, cache_fox_v,
              cache_fox_logf, state_conv_a, state_conv_d, state_lru, ln_g, ln_b, ffn_w13, ffn_w2, w_in,
              fox_b_f, conv_a_w, conv_d_w, conv_d_b, lru_wa, lru_ba, lru_wx, lru_bx, lru_lambda, w_out):
    dt = x_prompt.dtype
    B = x_prompt.shape[0]
    empty_past = (
        jnp.zeros((DEPTH, B, 0, H_B, HEAD_DIM), dt),
        jnp.zeros((DEPTH, B, 0, H_B, HEAD_DIM), dt),
        jnp.zeros((DEPTH, B, 0, IDX_DIM), dt),
        jnp.zeros((DEPTH, B, 0, H_C, HEAD_DIM), dt),
        jnp.zeros((DEPTH, B, 0, H_C, HEAD_DIM), dt),
        jnp.zeros((DEPTH, B, 0, H_C), dt),
        jnp.zeros((DEPTH, B, CONV_A_W - 1, W_A), dt),
        jnp.zeros((DEPTH, B, CONV_D_W - 1, W_D), dt),
        jnp.zeros((DEPTH, B, W_D), dt),
    )
    sample_past = (cache_dsa_k, cache_dsa_v, cache_dsa_kidx, cache_fox_k, cache_fox_v, cache_fox_logf,
                   state_conv_a, state_conv_d, state_lru)
    y_prompt, p_state = trunk(x_prompt, empty_past, ln_g, ln_b, ffn_w13, ffn_w2, w_in, fox_b_f, conv_a_w,
                              conv_d_w, conv_d_b, lru_wa, lru_ba, lru_wx, lru_bx, lru_lambda, w_out)
    y_sample, s_state = trunk(x_sample, sample_past, ln_g, ln_b, ffn_w13, ffn_w2, w_in, fox_b_f, conv_a_w,
                              conv_d_w, conv_d_b, lru_wa, lru_ba, lru_wx, lru_bx, lru_lambda, w_out)
    p_dsa_k, p_dsa_v, p_dsa_kidx, p_fox_k, p_fox_v, p_fox_logf, p_conv_a, p_conv_d, p_lru = p_state
    s_dsa_k, s_dsa_v, s_dsa_kidx, s_fox_k, s_fox_v, s_fox_logf, s_conv_a, s_conv_d, s_lru = s_state
    return (y_prompt, y_sample, p_dsa_k, p_dsa_v, p_dsa_kidx, p_fox_k, p_fox_v, p_fox_logf, p_conv_a, p_conv_d, p_lru,
            s_dsa_k, s_dsa_v, s_dsa_kidx, s_fox_k, s_fox_v, s_fox_logf, s_conv_a, s_conv_d, s_lru)
```

```python
import math
import numpy as np
from contextlib import ExitStack
import concourse.bass as bass
import concourse.mybir as mybir
from concourse.bass_utils import run_bass_kernel_spmd

F32 = mybir.dt.float32
BF16 = mybir.dt.bfloat16
AF = mybir.ActivationFunctionType
ALU = mybir.AluOpType

D = 4096
SEQ = 2048
DEPTH = 2
DSEQ = 64
PAST = 1024
HD = 128
NH = 8
DFF = 8192
N_IN = 12376
O_AH, O_AB, O_AC, O_BQ, O_BK, O_BV, O_BQI, O_BKI, O_BWI, O_CQ, O_CK, O_CV, O_CF, O_DX, O_DG = (
    0, 1024, 2048, 3072, 4096, 5120, 6144, 7168, 7232, 7248, 8272, 9296, 10320, 10328, 11352)
ALPHA = (2.0 * DEPTH) ** 0.25
LN_EPS = 1e-5
TOPK = 256
TT = 256
NEG = -1.0e30
KCS = 8
NSLOT = 3
SLABW = 4096


class Sched:
    ENG = ('pe', 'act', 'dve', 'pool')

    def __init__(self, nc, es):
        self.nc, self.es = nc, es
        self.plan = False
        self.eo = {'pe': nc.tensor, 'act': nc.scalar, 'dve': nc.vector, 'pool': nc.gpsimd, 'sp': nc.sync}
        self.nsem = 0
        self.cur = {e: [self.newsem(), 0] for e in self.ENG}
        self.waited = {}
        self.lastw = {}
        self.readers = {}
        self.dpool = {}
        self.dman = {}
        self.final = []

    def newsem(self):
        self.nsem += 1
        return self.es.enter_context(self.nc.semaphore(f"sm{self.nsem}"))

    def _wait(self, waiter, ev):
        sem, val, src = ev
        if src == 'pe':
            if waiter == 'pe':
                return
            assert not (sem is self.cur['pe'][0] and val > self.cur['pe'][1]), "wait on pending PE event"
        k = (waiter, id(sem))
        if self.waited.get(k, 0) >= val:
            return
        self.waited[k] = val
        self.eo[waiter].wait_ge(sem, val)

    def _deps(self, waiter, r, w, wa=()):
        for k in wa:
            for ev in self.readers.get(k, {}).values():
                self._wait(waiter, ev)
        for k in r:
            for ev in self.lastw.get(k, ()):
                self._wait(waiter, ev)
        for k in w:
            for ev in self.lastw.get(k, ()):
                self._wait(waiter, ev)
            for ev in self.readers.get(k, {}).values():
                self._wait(waiter, ev)

    def _record(self, ev, src, r, w, wa=()):
        for k in wa:
            self.lastw.setdefault(k, []).append(ev)
        for k in w:
            self.lastw[k] = [ev]
            self.readers[k] = {}
        for k in r:
            self.readers.setdefault(k, {})[src] = ev

    def op(self, eng, fn, r=(), w=(), inc=True, wa=()):
        if self.plan:
            return None
        self._deps(eng, r, w, wa)
        ins = fn()
        sem, cnt = self.cur[eng]
        if inc:
            cnt += 1
            ins.then_inc(sem, 1)
            self.cur[eng][1] = cnt
            ev = (sem, cnt, eng)
            if cnt >= 30000:
                self.cur[eng] = [self.newsem(), 0]
        else:
            ev = (sem, cnt + 1, eng)
        self._record(ev, eng, r, w, wa)
        return ev

    def pe(self, fn, r=(), w=(), inc=True):
        return self.op('pe', fn, r, w, inc)

    def act(self, fn, r=(), w=()):
        return self.op('act', fn, r, w)

    def dve(self, fn, r=(), w=()):
        return self.op('dve', fn, r, w)

    def pool(self, fn, r=(), w=()):
        return self.op('pool', fn, r, w)

    def dma(self, q, out, in_, r=(), w=(), wa=(), **kw):
        if self.plan:
            return None
        if q not in self.dpool:
            self.dpool[q] = [[self.newsem(), 0] for _ in range(8)]
            self.dman[q] = 0
        n = self.dman[q]
        self.dman[q] += 1
        slot = self.dpool[q][n % 8]
        if slot[1] >= 30000:
            self.final.append((slot[0], slot[1]))
            slot[0], slot[1] = self.newsem(), 0
        sem, cnt = slot
        if cnt > 0:
            self._wait(q, (sem, cnt, 'dma'))
        self._deps(q, r, w, wa)
        self.eo[q].dma_start(out=out, in_=in_, **kw).then_inc(sem, 16)
        slot[1] = cnt + 16
        ev = (sem, cnt + 16, ('dma', q, n))
        self._record(ev, ('dma', q, n), r, w, wa)
        return ev

    def finish(self):
        for q, pool in self.dpool.items():
            for sem, cnt in pool:
                if cnt > 0:
                    self._wait('sp', (sem, cnt, 'dma'))
        for sem, cnt in self.final:
            self._wait('sp', (sem, cnt, 'dma'))


class WStream:
    def __init__(self, S, ring):
        self.S, self.ring = S, ring
        self.reqs = []
        self.i = 0
        self.issued = 0

    def get(self, w_ap, k0, kcn, ranges):
        if self.S.plan:
            self.reqs.append((w_ap, k0, kcn, tuple(ranges)))
            return None, None
        i = self.i
        self.i += 1
        assert self.reqs[i][1:] == (k0, kcn, tuple(ranges))
        while self.issued < min(len(self.reqs), i + NSLOT):
            self._issue(self.issued)
            self.issued += 1
        slot = i % NSLOT
        W = sum(wd for _, wd in ranges)
        view = self.ring[:, slot, 0:kcn * W].rearrange("p (k n) -> p k n", n=W)
        return view, slot

    def _issue(self, j):
        w_ap, k0, kcn, ranges = self.reqs[j]
        slot = j % NSLOT
        W = sum(wd for _, wd in ranges)
        assert kcn * W <= SLABW
        view = self.ring[:, slot, 0:kcn * W].rearrange("p (k n) -> p k n", n=W)
        off = 0
        for ri, (c0, wd) in enumerate(ranges):
            self.S.dma('pool', out=view[:, :, off:off + wd],
                       in_=w_ap[k0 * 128:(k0 + kcn) * 128, c0:c0 + wd].rearrange("(k p) n -> p k n", p=128),
                       w=[('slab', slot)] if ri == 0 else [], wa=[('slab', slot)] if ri else [])
            off += wd


def build_program():
    nc = bass.Bass("TRN2", target_bir_lowering=False)
    es = ExitStack()

    def din(name, shape):
        return nc.dram_tensor(name, list(shape), F32, kind="ExternalInput").ap()

    def dout(name, shape):
        return nc.dram_tensor("o_" + name, list(shape), F32, kind="ExternalOutput").ap()

    def dscr(name, shape, dt=BF16):
        return nc.dram_tensor(name, list(shape), dt, kind="Internal").ap()

    I = {}
    I['xp'] = din('xp', (SEQ, D))
    I['xs'] = din('xs', (DSEQ, D))
    for nm in ('cdk', 'cdv', 'cfk', 'cfv'):
        I[nm] = din(nm, (DEPTH, PAST, 1024))
    I['cdi'] = din('cdi', (DEPTH, PAST, 64))
    I['cfl'] = din('cfl', (DEPTH, PAST, 8))
    I['sca'] = din('sca', (DEPTH, 2, 1024))
    I['scd'] = din('scd', (DEPTH, 3, 1024))
    I['slru'] = din('slru', (DEPTH, 1024))
    I['ln_g'] = din('ln_g', (DEPTH, 3, D))
    I['ln_b'] = din('ln_b', (DEPTH, 3, D))
    I['w13'] = din('w13', (DEPTH, 2, D, 2 * DFF))
    I['w2'] = din('w2', (DEPTH, 2, DFF, D))
    I['w_in'] = din('w_in', (DEPTH, D, N_IN))
    I['fox_b_f'] = din('fox_b_f', (DEPTH, 8))
    I['conv_a_w'] = din('conv_a_w', (DEPTH, 3, 1024))
    I['conv_d_w'] = din('conv_d_w', (DEPTH, 4, 1024))
    I['conv_d_b'] = din('conv_d_b', (DEPTH, 1024))
    I['lru_wa'] = din('lru_wa', (DEPTH, 8, 128, 128))
    I['lru_ba'] = din('lru_ba', (DEPTH, 1024))
    I['lru_wx'] = din('lru_wx', (DEPTH, 8, 128, 128))
    I['lru_bx'] = din('lru_bx', (DEPTH, 1024))
    I['lru_lambda'] = din('lru_lambda', (DEPTH, 1024))
    I['w_out'] = din('w_out', (DEPTH, D, D))
    I['tq'] = din('tq', (SEQ, 128))
    I['ti'] = din('ti', (SEQ, 128))

    O = {}
    for pre, T in (('p', SEQ), ('s', DSEQ)):
        O[pre + 'y'] = dout(pre + 'y', (T, D))
        for nm in ('dk', 'dv', 'fk', 'fv'):
            O[pre + nm] = dout(pre + nm, (DEPTH, T, 1024))
        O[pre + 'di'] = dout(pre + 'di', (DEPTH, T, 64))
        O[pre + 'fl'] = dout(pre + 'fl', (DEPTH, T, 8))
        O[pre + 'ca'] = dout(pre + 'ca', (DEPTH, 2, 1024))
        O[pre + 'cd'] = dout(pre + 'cd', (DEPTH, 3, 1024))
        O[pre + 'lru'] = dout(pre + 'lru', (DEPTH, 1024))

    SC = {}
    for pre, T in (('p', SEQ), ('s', PAST + DSEQ)):
        for br in ('d', 'f'):
            SC[pre + 'kT' + br] = dscr(pre + 'kT' + br, (DEPTH, NH, 128, T))
            SC[pre + 'v' + br] = dscr(pre + 'v' + br, (DEPTH, T, 1024))

    with es:
        S = Sched(nc, es)

        def sb(name, shape, dt=F32):
            return es.enter_context(nc.sbuf_tensor(name, list(shape), dt))

        xres = sb('xres', (128, 2, D))
        xT = sb('xT', (128, 32, TT), BF16)
        hT = sb('hT', (128, 64, TT), BF16)
        ring = sb('ring', (128, NSLOT, SLABW), BF16)
        ident = sb('ident', (128, 128))
        ones_f = sb('ones_f', (128, 128))
        ones_b = sb('ones_b', (128, 128), BF16)
        tri_f = sb('tri_f', (128, 128))
        eps_t = sb('eps_t', (128, 1))
        gblk = sb('gblk', (128, 2, 512))
        bblk = sb('bblk', (128, 2, 512))
        lnst = sb('lnst', (128, 2, 8, 6))
        lnmv = sb('lnmv', (128, 2, 2))
        lnrs = sb('lnrs', (128, 2, 2))
        siltmp = sb('siltmp', (128, 2, TT))
        stage = sb('stage', (128, 2, 256))
        stgb = sb('stgb', (128, 2, 256), BF16)
        ropet = sb('ropet', (128, 4, 64))
        cstq = sb('cstq', (128, 2, 128))
        csti = sb('csti', (128, 2, 128))
        kiT = sb('kiT', (128, DEPTH, SEQ), BF16)
        ki2 = sb('ki2', (128, 128))
        wis = sb('wis', (128, 2, 16))
        qiT = sb('qiT', (128, 8, TT), BF16)
        score = sb('score', (128, SEQ))
        work = sb('work', (128, SEQ))
        m8 = sb('m8', (128, 8))
        rl = sb('rl', (128, 2, 512))
        maskT = sb('maskT', (128, 2, 16, 128), BF16)
        KTa = sb('KTa', (128, 2, SEQ), BF16)
        Va = sb('Va', (128, 16, 256), BF16)
        qTs = sb('qTs', (128, 2, TT), BF16)
        kTs = sb('kTs', (128, 2, TT), BF16)
        PT = sb('PT', (128, 2, 128), BF16)
        rden = sb('rden', (128, 2, 128))
        Fall = sb('Fall', (128, DEPTH, 16, 8))
        Fcar = sb('Fcar', (128, DEPTH, 8))
        Fref = sb('Fref', (128, 2, 8))
        fbias = sb('fbias', (128, 2, 8))
        lfs = sb('lfs', (128, 2, 8))
        bfb = sb('bfb', (128, DEPTH, 8))
        cva = sb('cva', (128, DEPTH, 8, 2))
        cvd = sb('cvd', (128, DEPTH, 8, 3))
        hl = sb('hl', (128, DEPTH, 8))
        caw = sb('caw', (128, DEPTH, 3, 8))
        cdw = sb('cdw', (128, DEPTH, 4, 8))
        cdb = sb('cdb', (128, DEPTH, 8))
        lba = sb('lba', (128, DEPTH, 8))
        lbx = sb('lbx', (128, DEPTH, 8))
        lcp = sb('lcp', (128, DEPTH, 8))
        lcp2 = sb('lcp2', (128, DEPTH, 8))
        lwa = sb('lwa', (128, 2, 128), BF16)
        lwx = sb('lwx', (128, 2, 128), BF16)
        ubuf = sb('ubuf', (128, TT + 4))
        ab1 = sb('ab1', (128, TT))
        ab2 = sb('ab2', (128, TT))
        ab3 = sb('ab3', (128, TT))
        ab4 = sb('ab4', (128, TT))
        ab5 = sb('ab5', (128, TT))
        xcb = sb('xcb', (128, TT), BF16)
        ps = es.enter_context(nc.psum_tensor('ps', [128, 8, 512], F32))

        ws = WStream(S, ring)
        cnt = {'aux': 0, 'dense': 0, 'gb': 0, 'sil': 0, 'stg': 0, 'pt': 0, 'rl': 0, 'lw': 0}

        def setup_consts():
            S.pool(lambda: nc.gpsimd.memset(ident[:], 0.0), w=['ident'])
            S.pool(lambda: nc.gpsimd.affine_select(out=ident[:], in_=ident[:], pattern=[[-1, 128]], base=0,
                                                   channel_multiplier=1, compare_op=ALU.not_equal, fill=1.0),
                   r=['ident'], w=['ident'])
            S.pool(lambda: nc.gpsimd.memset(ones_f[:], 1.0), w=['ones_f'])
            S.pool(lambda: nc.gpsimd.memset(ones_b[:], 1.0), w=['ones_b'])
            S.pool(lambda: nc.gpsimd.memset(eps_t[:], LN_EPS), w=['eps'])
            S.pool(lambda: nc.gpsimd.memset(tri_f[:], 1.0), w=['tri'])
            S.pool(lambda: nc.gpsimd.affine_select(out=tri_f[:], in_=tri_f[:], pattern=[[1, 128]], base=0,
                                                   channel_multiplier=-1, compare_op=ALU.is_ge, fill=0.0),
                   r=['tri'], w=['tri'])
            nc_ = nc
            with nc_.allow_non_contiguous_dma(reason="tiny per-channel parameter loads"):
                for l in range(DEPTH):
                    S.dma('sp', out=bfb[:, l, :], in_=I['fox_b_f'][l].partition_broadcast(128), wa=['bfb'])
                    for i in range(3):
                        S.dma('sp', out=caw[:, l, i, :], in_=I['conv_a_w'][l, i].rearrange("(j p) -> p j", p=128), wa=['caw'])
                    for i in range(4):
                        S.dma('sp', out=cdw[:, l, i, :], in_=I['conv_d_w'][l, i].rearrange("(j p) -> p j", p=128), wa=['cdw'])
                    S.dma('sp', out=cdb[:, l, :], in_=I['conv_d_b'][l].rearrange("(j p) -> p j", p=128), wa=['cdb'])
                    S.dma('sp', out=lba[:, l, :], in_=I['lru_ba'][l].rearrange("(j p) -> p j", p=128), wa=['lba'])
                    S.dma('sp', out=lbx[:, l, :], in_=I['lru_bx'][l].rearrange("(j p) -> p j", p=128), wa=['lbx'])
                    S.dma('sp', out=lcp[:, l, :], in_=I['lru_lambda'][l].rearrange("(j p) -> p j", p=128), wa=['lcp'])
            S.act(lambda: nc.scalar.activation(out=lcp[:], in_=lcp[:], func=AF.Exp, scale=-1.0), r=['lcp'], w=['lcp'])
            S.act(lambda: nc.scalar.activation(out=lcp[:], in_=lcp[:], func=AF.Ln, bias=1.0, scale=1.0), r=['lcp'], w=['lcp'])
            S.dve(lambda: nc.vector.tensor_scalar(out=lcp2[:], in0=lcp[:], scalar1=-16.0, scalar2=None, op0=ALU.mult),
                  r=['lcp'], w=['lcp2'])
            S.dve(lambda: nc.vector.tensor_scalar(out=lcp[:], in0=lcp[:], scalar1=-8.0, scalar2=None, op0=ALU.mult),
                  r=['lcp', 'lcp2'], w=['lcp'])

        def xkeys(chunks):
            return [('xT', ci) for ci in range(len(chunks))]

        def dense_tok(inT, in_keys, nkc, chunks, w_ap, col0, ncols, consume, allbanks=False):
            base = (cnt['dense'] % (4 if allbanks else 2)) * 2
            cnt['dense'] += 1
            banks = [base + ci for ci in range(len(chunks))]
            for s0 in range(0, nkc, KCS):
                kcn = min(KCS, nkc - s0)
                slab, slot = ws.get(w_ap, s0, kcn, [(col0, ncols)])
                for kk in range(kcn):
                    kc = s0 + kk
                    for ci, (c0, cn) in enumerate(chunks):
                        last = (kc == nkc - 1)
                        S.pe(lambda: nc.tensor.matmul(ps[:cn, banks[ci], :ncols], lhsT=inT[:, kc, c0:c0 + cn],
                                                      rhs=slab[:, kk, :ncols], start=(kc == 0), stop=last),
                             r=[('slab', slot)] + in_keys(kc, ci), w=[('ps', banks[ci])],
                             inc=(last or (kk == kcn - 1 and ci == len(chunks) - 1)))
            for ci, (c0, cn) in enumerate(chunks):
                consume(ci, c0, cn, banks[ci])

        def dense_feat(inT, in_keys, nkc, ntok, w_ap, ranges, consume, allbanks=False):
            units = []
            for (c0, wd) in ranges:
                for u in range(wd // 128):
                    units.append((len(units)))
            nu = len(units)
            base = (cnt['dense'] % 2) * 4 if allbanks else 0
            cnt['dense'] += 1
            banks = [(base + u) % 8 for u in range(nu)]
            umap = []
            for ri, (c0, wd) in enumerate(ranges):
                for u in range(wd // 128):
                    umap.append(ri)
            for s0 in range(0, nkc, KCS):
                kcn = min(KCS, nkc - s0)
                slab, slot = ws.get(w_ap, s0, kcn, ranges)
                for kk in range(kcn):
                    kc = s0 + kk
                    for u in range(nu):
                        last = (kc == nkc - 1)
                        S.pe(lambda: nc.tensor.matmul(ps[:, banks[u], :ntok], lhsT=slab[:, kk, u * 128:(u + 1) * 128],
                                                      rhs=inT[:, kc, :ntok], start=(kc == 0), stop=last),
                             r=[('slab', slot)] + in_keys(kc, None), w=[('ps', banks[u])],
                             inc=(last or (kk == kcn - 1 and u == nu - 1)))
            consume(banks)

        def xT_keys(kc, ci):
            if ci is None:
                return [('xT', 0), ('xT', 1)]
            return [('xT', ci)]

        def hT_keys(kc, ci):
            return [('hT', kc)]

        def ffn(l, i, chunks, ntok):
            w13 = I['w13'][l, i]
            w2 = I['w2'][l, i]
            for mg in range(DFF // 256):
                def cons(banks, mg=mg):
                    for u in range(2):
                        m = mg * 2 + u
                        sl = cnt['sil'] % 2
                        cnt['sil'] += 1
                        S.act(lambda: nc.scalar.activation(out=siltmp[:, sl, :ntok], in_=ps[:, banks[u], :ntok], func=AF.Silu),
                              r=[('ps', banks[u])], w=[('sil', sl)])
                        S.dve(lambda: nc.vector.tensor_tensor(out=hT[:, m, :ntok], in0=ps[:, banks[2 + u], :ntok],
                                                              in1=siltmp[:, sl, :ntok], op=ALU.mult),
                              r=[('ps', banks[2 + u]), ('sil', sl)], w=[('hT', m)])
                dense_feat(xT, xT_keys, 32, ntok, w13, [(mg * 256, 256), (DFF + mg * 256, 256)], cons, allbanks=True)
            for dg in range(8):
                def cons2(ci, c0, cn, bank, dg=dg):
                    S.dve(lambda: nc.vector.scalar_tensor_tensor(out=xres[:cn, ci, dg * 512:(dg + 1) * 512], in0=ps[:cn, bank, :],
                                                                 scalar=0.5, in1=xres[:cn, ci, dg * 512:(dg + 1) * 512],
                                                                 op0=ALU.mult, op1=ALU.add),
                          r=[('ps', bank), ('x', ci, dg)], w=[('x', ci, dg)])
                dense_tok(hT, hT_keys, 64, chunks, w2, dg * 512, 512, cons2, allbanks=True)

        def layer_norm(l, i, chunks, out_scale):
            for ci, (c0, cn) in enumerate(chunks):
                for blk in range(8):
                    S.dve(lambda: nc.vector.bn_stats(out=lnst[:cn, ci, blk, :], in_=xres[:cn, ci, blk * 512:(blk + 1) * 512]),
                          r=[('x', ci, blk)], w=[('lnst', ci, blk)])
                S.dve(lambda: nc.vector.bn_aggr(out=lnmv[:cn, ci, :], in_=lnst[:cn, ci, :, :].rearrange("p a b -> p (a b)")),
                      r=[('lnst', ci, b_) for b_ in range(8)], w=[('lnmv', ci)])
                S.act(lambda: nc.scalar.activation(out=lnrs[:cn, ci, 0:1], in_=lnmv[:cn, ci, 1:2], func=AF.Sqrt,
                                                   bias=eps_t[:cn, :], scale=1.0),
                      r=[('lnmv', ci), 'eps'], w=[('lnrs', ci)])
                S.dve(lambda: nc.vector.reciprocal(out=lnrs[:cn, ci, 0:1], in_=lnrs[:cn, ci, 0:1]), r=[('lnrs', ci)], w=[('lnrs', ci)])
                S.dve(lambda: nc.vector.tensor_scalar(out=lnrs[:cn, ci, 0:1], in0=lnrs[:cn, ci, 0:1], scalar1=float(out_scale),
                                                      scalar2=None, op0=ALU.mult), r=[('lnrs', ci)], w=[('lnrs', ci)])
                S.dve(lambda: nc.vector.tensor_scalar(out=lnrs[:cn, ci, 1:2], in0=lnmv[:cn, ci, 0:1], scalar1=-1.0,
                                                      scalar2=lnrs[:cn, ci, 0:1], op0=ALU.mult, op1=ALU.mult),
                      r=[('lnrs', ci), ('lnmv', ci)], w=[('lnrs', ci)])
            for blk in range(8):
                gs = cnt['gb'] % 2
                cnt['gb'] += 1
                S.dma('sp', out=gblk[:, gs, :], in_=I['ln_g'][l, i, blk * 512:(blk + 1) * 512].partition_broadcast(128), w=[('g', gs)])
                S.dma('sp', out=bblk[:, gs, :], in_=I['ln_b'][l, i, blk * 512:(blk + 1) * 512].partition_broadcast(128), w=[('b', gs)])
                for ci, (c0, cn) in enumerate(chunks):
                    xb = xres[:cn, ci, blk * 512:(blk + 1) * 512]
                    S.act(lambda: nc.scalar.activation(out=xb, in_=xb, func=AF.Identity, bias=lnrs[:cn, ci, 1:2],
                                                       scale=lnrs[:cn, ci, 0:1]),
                          r=[('x', ci, blk), ('lnrs', ci)], w=[('x', ci, blk)])
                    S.dve(lambda: nc.vector.tensor_tensor(out=xb, in0=xb, in1=gblk[:cn, gs, :], op=ALU.mult),
                          r=[('x', ci, blk), ('g', gs)], w=[('x', ci, blk)])
                    S.dve(lambda: nc.vector.scalar_tensor_tensor(out=xb, in0=bblk[:cn, gs, :], scalar=float(out_scale), in1=xb,
                                                                 op0=ALU.mult, op1=ALU.add),
                          r=[('x', ci, blk), ('b', gs)], w=[('x', ci, blk)])
            make_xT(chunks, 1.0 / out_scale)

        def make_xT(chunks, scale):
            for ci, (c0, cn) in enumerate(chunks):
                for g4 in range(8):
                    bank = 4 + (cnt['aux'] % 4)
                    cnt['aux'] += 1
                    for j in range(4):
                        c = g4 * 4 + j
                        S.pe(lambda: nc.tensor.transpose(ps[:, bank, j * 128:j * 128 + cn], xres[:cn, ci, c * 128:(c + 1) * 128],
                                                         ident[:cn, :cn]),
                             r=[('x', ci, c // 4), 'ident'], w=[('ps', bank)], inc=(j == 3))
                    src = ps[:, bank, :].rearrange("p (j t) -> p j t", t=128)[:, :, :cn]
                    dst = xT[:, g4 * 4:(g4 + 1) * 4, c0:c0 + cn]
                    if g4 % 2 == 0:
                        S.act(lambda: nc.scalar.activation(out=dst, in_=src, func=AF.Identity, scale=float(scale)),
                              r=[('ps', bank)], w=[('xT', ci)])
                    else:
                        S.dve(lambda: nc.vector.tensor_scalar(out=dst, in0=src, scalar1=float(scale), scalar2=None, op0=ALU.mult),
                              r=[('ps', bank)], w=[('xT', ci)])

        def tok_group(l, chunks, col0, ncols, consume):
            dense_tok(xT, xT_keys, 32, chunks, I['w_in'][l], col0, ncols, consume)

        def rope_inplace(st, cn, nh, hd, half, tab, ci, sk, tk):
            v = st.rearrange("p (h d) -> p h d", d=hd)
            x1 = v[:, :, 0:half]
            x2 = v[:, :, half:2 * half]
            n = nh * half
            cosv = tab[:cn, ci, 0:n].rearrange("p (h d) -> p h d", d=half)
            sinv = tab[:cn, ci, 64:64 + n].rearrange("p (h d) -> p h d", d=half)
            t = [ropet[:cn, k, 0:n].rearrange("p (h d) -> p h d", d=half) for k in range(4)]
            S.dve(lambda: nc.vector.tensor_tensor(out=t[0], in0=x1, in1=cosv, op=ALU.mult), r=[sk, tk], w=['ropet'])
            S.dve(lambda: nc.vector.tensor_tensor(out=t[1], in0=x2, in1=sinv, op=ALU.mult), r=[sk, tk], w=['ropet'])
            S.dve(lambda: nc.vector.tensor_tensor(out=t[2], in0=x2, in1=cosv, op=ALU.mult), r=[sk, tk], w=['ropet'])
            S.dve(lambda: nc.vector.tensor_tensor(out=t[3], in0=x1, in1=sinv, op=ALU.mult), r=[sk, tk], w=['ropet'])
            S.dve(lambda: nc.vector.tensor_tensor(out=x1, in0=t[0], in1=t[1], op=ALU.subtract), r=['ropet'], w=[sk])
            S.dve(lambda: nc.vector.tensor_tensor(out=x2, in0=t[2], in1=t[3], op=ALU.add), r=['ropet'], w=[sk])

        def transpose_to(dst_fn, src, cn, ncol128, key_r, key_w, dt_scale=1.0):
            for j0 in range(0, ncol128, 4):
                bank = 4 + (cnt['aux'] % 4)
                cnt['aux'] += 1
                nj = min(4, ncol128 - j0)
                for j in range(nj):
                    S.pe(lambda: nc.tensor.transpose(ps[:, bank, j * 128:j * 128 + cn], src[:, (j0 + j) * 128:(j0 + j + 1) * 128],
                                                     ident[:cn, :cn]),
                         r=key_r + ['ident'], w=[('ps', bank)], inc=(j == nj - 1))
                for j in range(nj):
                    d = dst_fn(j0 + j)
                    S.act(lambda: nc.scalar.activation(out=d, in_=ps[:, bank, j * 128:j * 128 + cn], func=AF.Identity, scale=1.0),
                          r=[('ps', bank)], w=key_w)

        def mixer(sq, l, t0, chunks, ntok):
            pre, past = sq['pre'], sq['past']
            pos0 = past + t0
            gch0 = pos0 // 128
            nq = len(chunks)
            for ci, (c0, cn) in enumerate(chunks):
                S.dma('sp', out=cstq[:cn, ci, :], in_=I['tq'][pos0 + c0:pos0 + c0 + cn, :], w=[('tabq', ci)])
                S.dma('sp', out=csti[:cn, ci, :], in_=I['ti'][pos0 + c0:pos0 + c0 + cn, :], w=[('tabi', ci)])

            def newstage():
                s_ = cnt['stg'] % 2
                cnt['stg'] += 1
                return s_

            def cons_ki(ci, c0, cn, bank):
                S.act(lambda: nc.scalar.activation(out=ki2[:cn, 0:64], in_=ps[:cn, bank, 0:64], func=AF.Identity, scale=1.0),
                      r=[('ps', bank)], w=['ki2'])
                S.act(lambda: nc.scalar.activation(out=wis[:cn, ci, :], in_=ps[:cn, bank, 64:80], func=AF.Identity, scale=1.0),
                      r=[('ps', bank)], w=[('wis', ci)])
                rope_inplace(ki2[:cn, 0:64], cn, 1, 64, 8, csti, ci, 'ki2', ('tabi', ci))
                S.dve(lambda: nc.vector.tensor_copy(out=ki2[:cn, 64:128], in_=ki2[:cn, 0:64]), r=['ki2'], w=['ki2'])
                S.dma('sp', out=O[pre + 'di'][l, t0 + c0:t0 + c0 + cn, :], in_=ki2[:cn, 0:64], r=['ki2'])
                transpose_to(lambda j: kiT[:, l, pos0 + c0:pos0 + c0 + cn], ki2[:cn, :], cn, 1, ['ki2'], [('kiT', l)])
            tok_group(l, chunks, O_BKI, 80, cons_ki)

            for g in range(2):
                def cons_qi(ci, c0, cn, bank, g=g):
                    dst = work[:cn, ci * 1024 + g * 512:ci * 1024 + (g + 1) * 512]
                    S.act(lambda: nc.scalar.activation(out=dst, in_=ps[:cn, bank, :], func=AF.Identity, scale=1.0),
                          r=[('ps', bank)], w=['work'])
                    rope_inplace(dst, cn, 8, 64, 8, csti, ci, 'work', ('tabi', ci))
                    transpose_to(lambda j: qiT[:, g * 4 + j, c0:c0 + cn], dst, cn, 4, ['work'], ['qiT'])
                tok_group(l, chunks, O_BQI + g * 512, 512, cons_qi)

            for qi_, (q0, qn) in enumerate(chunks):
                if pre == 'p':
                    s_end = pos0 + q0 + qn
                    gq = (pos0 + q0) // 128
                    use_topk = gq >= 2
                else:
                    s_end = past + DSEQ
                    use_topk = True
                for kb0 in range(0, s_end, 512):
                    w_ = min(512, s_end - kb0)
                    for h in range(16):
                        bank = 4 + (cnt['aux'] % 4)
                        cnt['aux'] += 1
                        pb = (h % 2) * 64
                        S.pe(lambda: nc.tensor.matmul(ps[:qn, bank, :w_], lhsT=qiT[pb:pb + 64, h // 2, q0:q0 + qn],
                                                      rhs=kiT[pb:pb + 64, l, kb0:kb0 + w_], start=True, stop=True),
                             r=['qiT', ('kiT', l)], w=[('ps', bank)])
                        rs_ = cnt['rl'] % 2
                        cnt['rl'] += 1
                        S.act(lambda: nc.scalar.activation(out=rl[:qn, rs_, :w_], in_=ps[:qn, bank, :w_], func=AF.Relu),
                              r=[('ps', bank)], w=[('rl', rs_)])
                        if h == 0:
                            S.dve(lambda: nc.vector.tensor_scalar(out=score[:qn, kb0:kb0 + w_], in0=rl[:qn, rs_, :w_],
                                                                  scalar1=wis[:qn, qi_, 0:1], scalar2=None, op0=ALU.mult),
                                  r=[('rl', rs_), ('wis', qi_)], w=['score'])
                        else:
                            S.dve(lambda: nc.vector.scalar_tensor_tensor(out=score[:qn, kb0:kb0 + w_], in0=rl[:qn, rs_, :w_],
                                                                         scalar=wis[:qn, qi_, h:h + 1], in1=score[:qn, kb0:kb0 + w_],
                                                                         op0=ALU.mult, op1=ALU.add),
                                  r=[('rl', rs_), ('wis', qi_), 'score'], w=['score'])
                if pre == 'p':
                    S.dve(lambda: nc.vector.memset(score[0:64, s_end - 64:s_end], NEG), r=['score'], w=['score'])
                if use_topk:
                    src = score
                    for rnd in range(TOPK // 8):
                        S.dve(lambda: nc.vector.max(out=m8[:qn, :], in_=src[:qn, 0:s_end]), r=['score', 'work'], w=['m8'])
                        if rnd < TOPK // 8 - 1:
                            S.dve(lambda: nc.vector.match_replace(out=work[:qn, 0:s_end], in_to_replace=m8[:qn, :],
                                                                  in_values=src[:qn, 0:s_end], imm_value=NEG),
                                  r=['score', 'work', 'm8'], w=['work'])
                            src = work
                    S.dve(lambda: nc.vector.tensor_scalar(out=work[:qn, 0:s_end], in0=score[:qn, 0:s_end], scalar1=m8[:qn, 7:8],
                                                          scalar2=None, op0=ALU.is_ge), r=['score', 'm8', 'work'], w=['work'])
                else:
                    S.dve(lambda: nc.vector.tensor_scalar(out=work[:qn, 0:s_end], in0=score[:qn, 0:s_end], scalar1=-1.0e29,
                                                          scalar2=None, op0=ALU.is_ge), r=['score', 'work'], w=['work'])
                nsc = (s_end + 127) // 128
                for sc in range(nsc):
                    sn = min(128, s_end - sc * 128)
                    bank = 4 + (cnt['aux'] % 4)
                    cnt['aux'] += 1
                    S.pe(lambda: nc.tensor.transpose(ps[:sn, bank, :qn], work[:qn, sc * 128:sc * 128 + sn], ident[:qn, :qn]),
                         r=['work', 'ident'], w=[('ps', bank)])
                    S.act(lambda: nc.scalar.activation(out=maskT[:sn, qi_, sc, :qn], in_=ps[:sn, bank, :qn], func=AF.Identity, scale=1.0),
                          r=[('ps', bank)], w=['maskT'])

            def cons_cf(ci, c0, cn, bank):
                S.dve(lambda: nc.vector.tensor_tensor(out=lfs[:cn, ci, :], in0=ps[:cn, bank, 0:8], in1=bfb[:cn, l, :], op=ALU.add),
                      r=[('ps', bank), 'bfb'], w=[('lfs', ci)])
                S.act(lambda: nc.scalar.activation(out=lfs[:cn, ci, :], in_=lfs[:cn, ci, :], func=AF.Exp, scale=-1.0),
                      r=[('lfs', ci)], w=[('lfs', ci)])
                S.act(lambda: nc.scalar.activation(out=lfs[:cn, ci, :], in_=lfs[:cn, ci, :], func=AF.Ln, bias=1.0, scale=1.0),
                      r=[('lfs', ci)], w=[('lfs', ci)])
                S.dve(lambda: nc.vector.tensor_scalar(out=lfs[:cn, ci, :], in0=lfs[:cn, ci, :], scalar1=-1.0, scalar2=None, op0=ALU.mult),
                      r=[('lfs', ci)], w=[('lfs', ci)])
                S.dma('sp', out=O[pre + 'fl'][l, t0 + c0:t0 + c0 + cn, :], in_=lfs[:cn, ci, :], r=[('lfs', ci)])
            tok_group(l, chunks, O_CF, 8, cons_cf)
            for ci, (c0, cn) in enumerate(chunks):
                cum_chunk(l, gch0 + ci, lfs[:cn, ci, :], cn, [('lfs', ci)])

            for br, oq, ok_, ov, ych in (('d', O_BQ, O_BK, O_BV, 8), ('f', O_CQ, O_CK, O_CV, 16)):
                for hg in range(4):
                    cs_ = slice(hg * 256, (hg + 1) * 256)

                    def cons_q(ci, c0, cn, bank):
                        s_ = newstage()
                        S.act(lambda: nc.scalar.activation(out=stage[:cn, s_, 0:256], in_=ps[:cn, bank, 0:256], func=AF.Identity, scale=1.0),
                              r=[('ps', bank)], w=[('stg', s_)])
                        if br == 'd':
                            rope_inplace(stage[:cn, s_, 0:256], cn, 2, 128, 16, cstq, ci, ('stg', s_), ('tabq', ci))
                        transpose_to(lambda j: qTs[:, j, c0:c0 + cn], stage[:cn, s_, 0:256], cn, 2, [('stg', s_)], ['qTs'])
                    tok_group(l, chunks, oq + hg * 256, 256, cons_q)

                    def cons_k(ci, c0, cn, bank):
                        s_ = newstage()
                        S.act(lambda: nc.scalar.activation(out=stage[:cn, s_, 0:256], in_=ps[:cn, bank, 0:256], func=AF.Identity, scale=1.0),
                              r=[('ps', bank)], w=[('stg', s_)])
                        if br == 'd':
                            rope_inplace(stage[:cn, s_, 0:256], cn, 2, 128, 16, cstq, ci, ('stg', s_), ('tabq', ci))
                        S.dma('sp', out=O[pre + br + 'k'][l, t0 + c0:t0 + c0 + cn, cs_], in_=stage[:cn, s_, 0:256], r=[('stg', s_)])
                        transpose_to(lambda j: kTs[:, j, c0:c0 + cn], stage[:cn, s_, 0:256], cn, 2, [('stg', s_)], ['kTs'])
                    tok_group(l, chunks, ok_ + hg * 256, 256, cons_k)
                    with nc.allow_non_contiguous_dma(reason="kT cache rows"):
                        S.dma('sp', out=SC[pre + 'kT' + br][l, hg * 2:(hg + 1) * 2, :, pos0:pos0 + ntok].rearrange("h d t -> d h t"),
                              in_=kTs[:, :, :ntok], r=['kTs'], wa=[('dkT', br, hg)])

                    def cons_v(ci, c0, cn, bank):
                        s_ = newstage()
                        S.act(lambda: nc.scalar.activation(out=stage[:cn, s_, 0:256], in_=ps[:cn, bank, 0:256], func=AF.Identity, scale=1.0),
                              r=[('ps', bank)], w=[('stg', s_)])
                        S.dma('sp', out=O[pre + br + 'v'][l, t0 + c0:t0 + c0 + cn, cs_], in_=stage[:cn, s_, 0:256], r=[('stg', s_)])
                        S.dve(lambda: nc.vector.tensor_copy(out=stgb[:cn, s_, 0:256], in_=stage[:cn, s_, 0:256]), r=[('stg', s_)], w=[('stgb', s_)])
                        S.dma('sp', out=SC[pre + 'v' + br][l, pos0 + c0:pos0 + c0 + cn, cs_], in_=stgb[:cn, s_, 0:256],
                              r=[('stgb', s_)], wa=[('dv', br, hg)])
                    tok_group(l, chunks, ov + hg * 256, 256, cons_v)

                    s_tot = pos0 + ntok
                    with nc.allow_non_contiguous_dma(reason="cache loads"):
                        S.dma('sp', out=KTa[:, :, 0:s_tot], in_=SC[pre + 'kT' + br][l, hg * 2:(hg + 1) * 2, :, 0:s_tot].rearrange("h d t -> d h t"),
                              r=[('dkT', br, hg)], w=['KTa'])
                        nfull = s_tot // 128
                        S.dma('sp', out=Va[:, 0:nfull, :],
                              in_=SC[pre + 'v' + br][l, 0:nfull * 128, cs_].rearrange("(c p) n -> p c n", p=128),
                              r=[('dv', br, hg)], w=['Va'])
                        if s_tot % 128:
                            rem = s_tot % 128
                            S.dma('sp', out=Va[:rem, nfull, :], in_=SC[pre + 'v' + br][l, nfull * 128:s_tot, cs_],
                                  r=[('dv', br, hg)], wa=['Va'])
                    for h4 in range(2):
                        h = hg * 2 + h4
                        for qi_, (q0, qn) in enumerate(chunks):
                            dsc = (gch0 + qi_) if pre == 'p' else (past // 128)
                            for sc in range(dsc + 1):
                                sn = min(128, s_tot - sc * 128)
                                pt_ = cnt['pt'] % 2
                                sb_ = 4 + pt_
                                cnt['pt'] += 1
                                S.pe(lambda: nc.tensor.matmul(ps[:sn, sb_, :qn], lhsT=KTa[:, h4, sc * 128:sc * 128 + sn],
                                                              rhs=qTs[:, h4, q0:q0 + qn], start=True, stop=True),
                                     r=['KTa', 'qTs'], w=[('ps', sb_)])
                                if br == 'f':
                                    S.dve(lambda: nc.vector.tensor_tensor(out=fbias[:sn, pt_, 0:1], in0=Fref[:sn, qi_, h:h + 1],
                                                                          in1=Fall[:sn, l, sc, h:h + 1], op=ALU.subtract),
                                          r=[('Fref', qi_), ('Fall', l)], w=[('fbias', pt_)])
                                    S.act(lambda: nc.scalar.activation(out=PT[:sn, pt_, :qn], in_=ps[:sn, sb_, :qn], func=AF.Exp,
                                                                       bias=fbias[:sn, pt_, 0:1], scale=HD ** -0.5),
                                          r=[('ps', sb_), ('fbias', pt_)], w=[('PT', pt_)])
                                    if sc == dsc:
                                        S.pool(lambda: nc.gpsimd.affine_select(out=PT[:sn, pt_, :qn], in_=PT[:sn, pt_, :qn],
                                                                               pattern=[[1, qn]], base=0, channel_multiplier=-1,
                                                                               compare_op=ALU.is_ge, fill=0.0),
                                               r=[('PT', pt_)], w=[('PT', pt_)])
                                else:
                                    S.act(lambda: nc.scalar.activation(out=PT[:sn, pt_, :qn], in_=ps[:sn, sb_, :qn], func=AF.Exp,
                                                                       scale=HD ** -0.5),
                                          r=[('ps', sb_)], w=[('PT', pt_)])
                                    S.dve(lambda: nc.vector.tensor_tensor(out=PT[:sn, pt_, :qn], in0=PT[:sn, pt_, :qn],
                                                                          in1=maskT[:sn, qi_, sc, :qn], op=ALU.mult),
                                          r=[('PT', pt_), 'maskT'], w=[('PT', pt_)])
                                S.pe(lambda: nc.tensor.matmul(ps[:, 6, :qn], lhsT=Va[:sn, sc, h4 * 128:(h4 + 1) * 128], rhs=PT[:sn, pt_, :qn],
                                                              start=(sc == 0), stop=(sc == dsc)),
                                     r=['Va', ('PT', pt_)], w=[('ps', 6)])
                                S.pe(lambda: nc.tensor.matmul(ps[:, 7, :qn], lhsT=ones_b[:sn, :], rhs=PT[:sn, pt_, :qn],
                                                              start=(sc == 0), stop=(sc == dsc)),
                                     r=['ones_b', ('PT', pt_)], w=[('ps', 7)])
                            rd = cnt['rl'] % 2
                            cnt['rl'] += 1
                            S.dve(lambda: nc.vector.reciprocal(out=rden[:, rd, :qn], in_=ps[:, 7, :qn]), r=[('ps', 7)], w=[('rden', rd)])
                            S.dve(lambda: nc.vector.tensor_tensor(out=hT[:, ych + h, q0:q0 + qn], in0=ps[:, 6, :qn], in1=rden[:, rd, :qn],
                                                                  op=ALU.mult),
                                  r=[('ps', 6), ('rden', rd)], w=[('hT', ych + h)])

            for j in range(8):
                def cons_a(banks, j=j):
                    bh, bb, bc = banks[0], banks[1], banks[2]
                    S.act(lambda: nc.scalar.activation(out=ab1[:, :ntok], in_=ps[:, bh, :ntok], func=AF.Identity, scale=1.0),
                          r=[('ps', bh)], w=['ab1'])
                    S.dve(lambda: nc.vector.tensor_copy(out=ubuf[:, 0:2], in_=cva[:, l, j, :]), r=[('cva', l)], w=['ubuf'])
                    S.dve(lambda: nc.vector.tensor_tensor(out=ubuf[:, 2:2 + ntok], in0=ps[:, bc, :ntok], in1=ab1[:, :ntok], op=ALU.mult),
                          r=[('ps', bc), 'ab1', 'ubuf'], w=['ubuf'])
                    S.dve(lambda: nc.vector.tensor_copy(out=cva[:, l, j, :], in_=ubuf[:, ntok:ntok + 2]), r=['ubuf'], w=[('cva', l)])
                    S.dve(lambda: nc.vector.tensor_scalar(out=ab2[:, :ntok], in0=ubuf[:, 0:ntok], scalar1=caw[:, l, 0, j:j + 1],
                                                          scalar2=None, op0=ALU.mult), r=['ubuf', 'caw'], w=['ab2'])
                    for i in (1, 2):
                        S.dve(lambda: nc.vector.scalar_tensor_tensor(out=ab2[:, :ntok], in0=ubuf[:, i:i + ntok], scalar=caw[:, l, i, j:j + 1],
                                                                     in1=ab2[:, :ntok], op0=ALU.mult, op1=ALU.add),
                              r=['ubuf', 'caw', 'ab2'], w=['ab2'])
                    S.dve(lambda: nc.vector.tensor_tensor(out=hT[:, j, :ntok], in0=ps[:, bb, :ntok], in1=ab2[:, :ntok], op=ALU.mult),
                          r=[('ps', bb), 'ab2'], w=[('hT', j)])
                dense_feat(xT, xT_keys, 32, ntok, I['w_in'][l],
                           [(O_AH + j * 128, 128), (O_AB + j * 128, 128), (O_AC + j * 128, 128)], cons_a)

            for j in range(8):
                lw = cnt['lw'] % 2
                cnt['lw'] += 1
                S.dma('pool', out=lwa[:, lw, :], in_=I['lru_wa'][l, j], w=[('lwa', lw)])
                S.dma('pool', out=lwx[:, lw, :], in_=I['lru_wx'][l, j], w=[('lwx', lw)])

                def cons_d(banks, j=j, lw=lw):
                    bx, bg = banks[0], banks[1]
                    S.dve(lambda: nc.vector.tensor_copy(out=ubuf[:, 0:3], in_=cvd[:, l, j, :]), r=[('cvd', l)], w=['ubuf'])
                    S.act(lambda: nc.scalar.activation(out=ubuf[:, 3:3 + ntok], in_=ps[:, bx, :ntok], func=AF.Identity, scale=1.0),
                          r=[('ps', bx), 'ubuf'], w=['ubuf'])
                    S.dve(lambda: nc.vector.tensor_copy(out=cvd[:, l, j, :], in_=ubuf[:, ntok:ntok + 3]), r=['ubuf'], w=[('cvd', l)])
                    S.dve(lambda: nc.vector.tensor_scalar(out=ab1[:, :ntok], in0=ubuf[:, 0:ntok], scalar1=cdw[:, l, 0, j:j + 1],
                                                          scalar2=cdb[:, l, j:j + 1], op0=ALU.mult, op1=ALU.add),
                          r=['ubuf', 'cdw', 'cdb'], w=['ab1'])
                    for i in (1, 2, 3):
                        S.dve(lambda: nc.vector.scalar_tensor_tensor(out=ab1[:, :ntok], in0=ubuf[:, i:i + ntok], scalar=cdw[:, l, i, j:j + 1],
                                                                     in1=ab1[:, :ntok], op0=ALU.mult, op1=ALU.add),
                              r=['ubuf', 'cdw', 'ab1'], w=['ab1'])
                    S.dve(lambda: nc.vector.tensor_copy(out=xcb[:, :ntok], in_=ab1[:, :ntok]), r=['ab1'], w=['xcb'])
                    S.act(lambda: nc.scalar.activation(out=ab5[:, :ntok], in_=ps[:, bg, :ntok], func=AF.Identity, scale=1.0),
                          r=[('ps', bg)], w=['ab5'])
                    S.dve(lambda: nc.vector.tensor_tensor(out=ab4[:, :ntok], in0=ab5[:, :ntok], in1=ab5[:, :ntok], op=ALU.mult),
                          r=['ab5'], w=['ab4'])
                    S.dve(lambda: nc.vector.tensor_scalar(out=ab4[:, :ntok], in0=ab4[:, :ntok], scalar1=0.044715 * 1.5957691216,
                                                          scalar2=1.5957691216, op0=ALU.mult, op1=ALU.add), r=['ab4'], w=['ab4'])
                    S.dve(lambda: nc.vector.tensor_tensor(out=ab4[:, :ntok], in0=ab4[:, :ntok], in1=ab5[:, :ntok], op=ALU.mult),
                          r=['ab4', 'ab5'], w=['ab4'])
                    S.act(lambda: nc.scalar.activation(out=ab4[:, :ntok], in_=ab4[:, :ntok], func=AF.Sigmoid), r=['ab4'], w=['ab4'])
                    S.dve(lambda: nc.vector.tensor_tensor(out=ab5[:, :ntok], in0=ab4[:, :ntok], in1=ab5[:, :ntok], op=ALU.mult),
                          r=['ab4', 'ab5'], w=['ab5'])
                    S.pe(lambda: nc.tensor.matmul(ps[:, bx, :ntok], lhsT=lwa[:, lw, :], rhs=xcb[:, :ntok], start=True, stop=True),
                         r=[('lwa', lw), 'xcb'], w=[('ps', bx)])
                    S.pe(lambda: nc.tensor.matmul(ps[:, bg, :ntok], lhsT=lwx[:, lw, :], rhs=xcb[:, :ntok], start=True, stop=True),
                         r=[('lwx', lw), 'xcb'], w=[('ps', bg)])
                    S.act(lambda: nc.scalar.activation(out=ab2[:, :ntok], in_=ps[:, bx, :ntok], func=AF.Sigmoid, bias=lba[:, l, j:j + 1], scale=1.0),
                          r=[('ps', bx), 'lba'], w=['ab2'])
                    S.act(lambda: nc.scalar.activation(out=ab3[:, :ntok], in_=ps[:, bg, :ntok], func=AF.Sigmoid, bias=lbx[:, l, j:j + 1], scale=1.0),
                          r=[('ps', bg), 'lbx'], w=['ab3'])
                    S.act(lambda: nc.scalar.activation(out=ab4[:, :ntok], in_=ab2[:, :ntok], func=AF.Exp, scale=lcp2[:, l, j:j + 1]),
                          r=['ab2', 'lcp2'], w=['ab4'])
                    S.act(lambda: nc.scalar.activation(out=ab2[:, :ntok], in_=ab2[:, :ntok], func=AF.Exp, scale=lcp[:, l, j:j + 1]),
                          r=['ab2', 'lcp'], w=['ab2'])
                    S.dve(lambda: nc.vector.tensor_scalar(out=ab4[:, :ntok], in0=ab4[:, :ntok], scalar1=-1.0, scalar2=1.0,
                                                          op0=ALU.mult, op1=ALU.add), r=['ab4'], w=['ab4'])
                    S.dve(lambda: nc.vector.tensor_scalar(out=ab4[:, :ntok], in0=ab4[:, :ntok], scalar1=0.0, scalar2=None, op0=ALU.max),
                          r=['ab4'], w=['ab4'])
                    S.act(lambda: nc.scalar.activation(out=ab4[:, :ntok], in_=ab4[:, :ntok], func=AF.Sqrt), r=['ab4'], w=['ab4'])
                    S.dve(lambda: nc.vector.tensor_tensor(out=ab4[:, :ntok], in0=ab4[:, :ntok], in1=ab3[:, :ntok], op=ALU.mult),
                          r=['ab4', 'ab3'], w=['ab4'])
                    S.dve(lambda: nc.vector.tensor_tensor(out=ab4[:, :ntok], in0=ab4[:, :ntok], in1=ab1[:, :ntok], op=ALU.mult),
                          r=['ab4', 'ab1'], w=['ab4'])
                    S.dve(lambda: nc.vector.tensor_tensor_scan(out=ab3[:, :ntok], data0=ab2[:, :ntok], data1=ab4[:, :ntok],
                                                               initial=hl[:, l, j:j + 1], op0=ALU.mult, op1=ALU.add),
                          r=['ab2', 'ab4', ('hl', l)], w=['ab3'])
                    S.dve(lambda: nc.vector.tensor_copy(out=hl[:, l, j:j + 1], in_=ab3[:, ntok - 1:ntok]), r=['ab3'], w=[('hl', l)])
                    S.dve(lambda: nc.vector.tensor_tensor(out=hT[:, 24 + j, :ntok], in0=ab3[:, :ntok], in1=ab5[:, :ntok], op=ALU.mult),
                          r=['ab3', 'ab5'], w=[('hT', 24 + j)])
                dense_feat(xT, xT_keys, 32, ntok, I['w_in'][l], [(O_DX + j * 128, 128), (O_DG + j * 128, 128)], cons_d)

        def cum_chunk(l, gc, lf_ap, cn, keys):
            bank = 4 + (cnt['aux'] % 4)
            cnt['aux'] += 1
            S.pe(lambda: nc.tensor.matmul(ps[:cn, bank, 0:8], lhsT=tri_f[:cn, :cn], rhs=lf_ap, start=True, stop=True),
                 r=keys + ['tri'], w=[('ps', bank)])
            S.dve(lambda: nc.vector.tensor_tensor(out=Fall[:cn, l, gc, :], in0=ps[:cn, bank, 0:8], in1=Fcar[:cn, l, :], op=ALU.add),
                  r=[('ps', bank), ('Fcar', l)], w=[('Fall', l)])
            bank2 = 4 + (cnt['aux'] % 4)
            cnt['aux'] += 1
            S.pe(lambda: nc.tensor.matmul(ps[:, bank2, 0:8], lhsT=ones_f[:cn, :], rhs=lf_ap, start=True, stop=True),
                 r=keys + ['ones_f'], w=[('ps', bank2)])
            qi_ = gc % 2 if cn == 128 else 0
            S.dve(lambda: nc.vector.tensor_copy(out=Fref[:, qi_, :], in_=Fcar[:, l, :]), r=[('Fcar', l)], w=[('Fref', qi_)])
            S.dve(lambda: nc.vector.tensor_tensor(out=Fcar[:, l, :], in0=ps[:, bank2, 0:8], in1=Fcar[:, l, :], op=ALU.add),
                  r=[('ps', bank2), ('Fcar', l), ('Fref', qi_)], w=[('Fcar', l)])

        def init_seq(sq):
            pre, past = sq['pre'], sq['past']
            for l in range(DEPTH):
                S.dve(lambda: nc.vector.memset(Fcar[:, l, :], 0.0), r=[('Fcar', l)], w=[('Fcar', l)])
                if pre == 'p':
                    S.dve(lambda: nc.vector.memset(cva[:, l, :, :], 0.0), r=[('cva', l)], w=[('cva', l)])
                    S.dve(lambda: nc.vector.memset(cvd[:, l, :, :], 0.0), r=[('cvd', l)], w=[('cvd', l)])
                    S.dve(lambda: nc.vector.memset(hl[:, l, :], 0.0), r=[('hl', l)], w=[('hl', l)])
                else:
                    with nc.allow_non_contiguous_dma(reason="tiny state loads"):
                        for i in range(2):
                            S.dma('sp', out=cva[:, l, :, i], in_=I['sca'][l, i].rearrange("(j p) -> p j", p=128), wa=[('cva', l)])
                        for i in range(3):
                            S.dma('sp', out=cvd[:, l, :, i], in_=I['scd'][l, i].rearrange("(j p) -> p j", p=128), wa=[('cvd', l)])
                        S.dma('sp', out=hl[:, l, :], in_=I['slru'][l].rearrange("(j p) -> p j", p=128), wa=[('hl', l)])
                    for br, kn, vn in (('d', 'cdk', 'cdv'), ('f', 'cfk', 'cfv')):
                        for hg in range(4):
                            cs_ = slice(hg * 256, (hg + 1) * 256)
                            S.dma('pool', out=SC['sv' + br][l, 0:PAST, cs_], in_=I[vn][l, :, cs_], wa=[('dv', br, hg)])
                            for c in range(PAST // 128):
                                s_ = cnt['stg'] % 2
                                cnt['stg'] += 1
                                S.dma('sp', out=stage[:, s_, 0:256], in_=I[kn][l, c * 128:(c + 1) * 128, cs_], w=[('stg', s_)])
                                transpose_to(lambda j: kTs[:, j, 0:128], stage[:, s_, 0:256], 128, 2, [('stg', s_)], ['kTs'])
                                with nc.allow_non_contiguous_dma(reason="kT cache rows"):
                                    S.dma('sp', out=SC['skT' + br][l, hg * 2:(hg + 1) * 2, :, c * 128:(c + 1) * 128].rearrange("h d t -> d h t"),
                                          in_=kTs[:, :, 0:128], r=['kTs'], wa=[('dkT', br, hg)])
                    for c in range(PAST // 128):
                        S.dma('sp', out=ki2[:, 0:64], in_=I['cdi'][l, c * 128:(c + 1) * 128, :], w=['ki2'])
                        S.dve(lambda: nc.vector.tensor_copy(out=ki2[:, 64:128], in_=ki2[:, 0:64]), r=['ki2'], w=['ki2'])
                        transpose_to(lambda j: kiT[:, l, c * 128:(c + 1) * 128], ki2[:, :], 128, 1, ['ki2'], [('kiT', l)])
                        S.dma('sp', out=lfs[:, 0, :], in_=I['cfl'][l, c * 128:(c + 1) * 128, :], w=[('lfs', 0)])
                        cum_chunk(l, c, lfs[:, 0, :], 128, [('lfs', 0)])

        def final_states(sq):
            pre = sq['pre']
            with nc.allow_non_contiguous_dma(reason="tiny state stores"):
                for l in range(DEPTH):
                    for i in range(2):
                        S.dma('sp', out=O[pre + 'ca'][l, i].rearrange("(j p) -> p j", p=128), in_=cva[:, l, :, i], r=[('cva', l)])
                    for i in range(3):
                        S.dma('sp', out=O[pre + 'cd'][l, i].rearrange("(j p) -> p j", p=128), in_=cvd[:, l, :, i], r=[('cvd', l)])
                    S.dma('sp', out=O[pre + 'lru'][l].rearrange("(j p) -> p j", p=128), in_=hl[:, l, :], r=[('hl', l)])

        def emit():
            cnt.update({k: 0 for k in cnt})
            setup_consts()
            seqs = [dict(pre='p', T=SEQ, past=0, x=I['xp'], y=O['py']),
                    dict(pre='s', T=DSEQ, past=PAST, x=I['xs'], y=O['sy'])]
            for sq in seqs:
                init_seq(sq)
                T = sq['T']
                for t0 in range(0, T, TT):
                    ntok = min(TT, T - t0)
                    chunks = [(c0, min(128, ntok - c0)) for c0 in range(0, ntok, 128)]
                    for ci, (c0, cn) in enumerate(chunks):
                        S.dma('sp', out=xres[:cn, ci, :], in_=sq['x'][t0 + c0:t0 + c0 + cn, :], w=[('x', ci, b_) for b_ in range(8)])
                    make_xT(chunks, 1.0)
                    for ci, (c0, cn) in enumerate(chunks):
                        for blk in range(8):
                            S.pool(lambda: nc.gpsimd.tensor_scalar(out=xres[:cn, ci, blk * 512:(blk + 1) * 512],
                                                                   in0=xres[:cn, ci, blk * 512:(blk + 1) * 512], scalar1=ALPHA, scalar2=None,
                                                                   op0=ALU.mult), r=[('x', ci, blk)], w=[('x', ci, blk)])
                    for l in range(DEPTH):
                        ffn(l, 0, chunks, ntok)
                        layer_norm(l, 0, chunks, ALPHA)
                        mixer(sq, l, t0, chunks, ntok)

                        for dg in range(8):
                            def cons_o(ci, c0, cn, bank, dg=dg):
                                S.dve(lambda: nc.vector.tensor_tensor(out=xres[:cn, ci, dg * 512:(dg + 1) * 512], in0=ps[:cn, bank, :],
                                                                      in1=xres[:cn, ci, dg * 512:(dg + 1) * 512], op=ALU.add),
                                      r=[('ps', bank), ('x', ci, dg)], w=[('x', ci, dg)])
                            dense_tok(hT, hT_keys, 32, chunks, I['w_out'][l], dg * 512, 512, cons_o, allbanks=True)
                        layer_norm(l, 1, chunks, ALPHA)
                        ffn(l, 1, chunks, ntok)
                        layer_norm(l, 2, chunks, ALPHA if l < DEPTH - 1 else 1.0)
                    for ci, (c0, cn) in enumerate(chunks):
                        S.dma('sp', out=sq['y'][t0 + c0:t0 + c0 + cn, :], in_=xres[:cn, ci, :], r=[('x', ci, b_) for b_ in range(8)])
                final_states(sq)
            S.finish()

        S.plan = True
        emit()
        S.plan = False
        emit()
    return nc


_ROPE_THETA = 500000.0


def _rope_tables():
    pos = np.arange(SEQ, dtype=np.float32)
    inv16 = (_ROPE_THETA ** (-np.arange(16, dtype=np.float32) / 16)).astype(np.float32)
    inv8 = (_ROPE_THETA ** (-np.arange(8, dtype=np.float32) / 8)).astype(np.float32)
    a16 = pos[:, None] * inv16[None, :]
    a8 = pos[:, None] * inv8[None, :]
    tq = np.concatenate([np.tile(np.cos(a16), (1, 4)), np.tile(np.sin(a16), (1, 4))], axis=1).astype(np.float32)
    ti = np.concatenate([np.tile(np.cos(a8), (1, 8)), np.tile(np.sin(a8), (1, 8))], axis=1).astype(np.float32)
    return np.ascontiguousarray(tq), np.ascontiguousarray(ti)


def kernel(x_prompt, x_sample, cache_dsa_k, cache_dsa_v, cache_dsa_kidx, cache_fox_k, cache_fox_v,
           cache_fox_logf, state_conv_a, state_conv_d, state_lru, ln_g, ln_b, ffn_w13, ffn_w2, w_in,
           fox_b_f, conv_a_w, conv_d_w, conv_d_b, lru_wa, lru_ba, lru_wx, lru_bx, lru_lambda, w_out):
    f = lambda a: np.ascontiguousarray(np.asarray(a, dtype=np.float32))
    nc = build_program()
    tq, ti = _rope_tables()
    shared = dict(ln_g=f(ln_g), ln_b=f(ln_b), w13=f(ffn_w13), w2=f(ffn_w2), w_in=f(w_in), fox_b_f=f(fox_b_f),
                  conv_a_w=f(conv_a_w), conv_d_w=f(conv_d_w), conv_d_b=f(conv_d_b), lru_wa=f(lru_wa), lru_ba=f(lru_ba),
                  lru_wx=f(lru_wx), lru_bx=f(lru_bx), lru_lambda=f(lru_lambda), w_out=f(w_out), tq=tq, ti=ti)
    in_maps = []
    for b in range(8):
        m = dict(shared)
        m['xp'] = f(x_prompt[b])
        m['xs'] = f(x_sample[b])
        m['cdk'] = f(np.asarray(cache_dsa_k)[:, b].reshape(DEPTH, PAST, 1024))
        m['cdv'] = f(np.asarray(cache_dsa_v)[:, b].reshape(DEPTH, PAST, 1024))
        m['cfk'] = f(np.asarray(cache_fox_k)[:, b].reshape(DEPTH, PAST, 1024))
        m['cfv'] = f(np.asarray(cache_fox_v)[:, b].reshape(DEPTH, PAST, 1024))
        m['cdi'] = f(np.asarray(cache_dsa_kidx)[:, b])
        m['cfl'] = f(np.asarray(cache_fox_logf)[:, b])
        m['sca'] = f(np.asarray(state_conv_a)[:, b])
        m['scd'] = f(np.asarray(state_conv_d)[:, b])
        m['slru'] = f(np.asarray(state_lru)[:, b])
        in_maps.append(m)
    res = run_bass_kernel_spmd(nc, in_maps, core_ids=list(range(8)))
    R = res.results

    def gather(name, shape_tail):
        return np.stack([np.asarray(R[b]['o_' + name], dtype=np.float32) for b in range(8)], axis=0)

    outs = [gather('py', None), gather('sy', None)]
    for pre, T in (('p', SEQ), ('s', DSEQ)):
        def st(name, tail):
            a = gather(pre + name, None)
            a = np.moveaxis(a, 0, 1)
            return np.ascontiguousarray(a.reshape((DEPTH, 8) + tail))
        outs += [st('dk', (T, NH, HD)), st('dv', (T, NH, HD)), st('di', (T, 64)), st('fk', (T, NH, HD)), st('fv', (T, NH, HD)),
                 st('fl', (T, NH)), st('ca', (2, 1024)), st('cd', (3, 1024)), st('lru', (1024,))]
    return tuple(outs)
```

```python
import math
import numpy as np
from contextlib import ExitStack
import concourse.bass as bass
import concourse.mybir as mybir
from concourse.bass_utils import run_bass_kernel_spmd

F32 = mybir.dt.float32
BF16 = mybir.dt.bfloat16
AF = mybir.ActivationFunctionType
ALU = mybir.AluOpType

D = 4096
SEQ = 2048
DEPTH = 2
DSEQ = 64
PAST = 1024
HD = 128
NH = 8
DFF = 8192
N_IN = 12376
O_AH, O_AB, O_AC, O_BQ, O_BK, O_BV, O_BQI, O_BKI, O_BWI, O_CQ, O_CK, O_CV, O_CF, O_DX, O_DG = (
    0, 1024, 2048, 3072, 4096, 5120, 6144, 7168, 7232, 7248, 8272, 9296, 10320, 10328, 11352)
ALPHA = (2.0 * DEPTH) ** 0.25
LN_EPS = 1e-5
TOPK = 256
TT = 256
NEG = -1.0e30
KCS = 8
NSLOT = 4
SLABW = 4096


class Sched:
    ENG = ('pe', 'act', 'dve', 'pool')

    def __init__(self, nc, es):
        self.nc, self.es = nc, es
        self.plan = False
        self.eo = {'pe': nc.tensor, 'act': nc.scalar, 'dve': nc.vector, 'pool': nc.gpsimd, 'sp': nc.sync}
        self.nsem = 0
        self.cur = {e: [self.newsem(), 0] for e in self.ENG}
        self.waited = {}
        self.lastw = {}
        self.readers = {}
        self.dpool = {}
        self.dman = {}
        self.final = []

    def newsem(self):
        self.nsem += 1
        return self.es.enter_context(self.nc.semaphore(f"sm{self.nsem}"))

    def _wait(self, waiter, ev):
        sem, val, src = ev
        if src == 'pe':
            if waiter == 'pe':
                return
            assert not (sem is self.cur['pe'][0] and val > self.cur['pe'][1]), "wait on pending PE event"
        k = (waiter, id(sem))
        if self.waited.get(k, 0) >= val:
            return
        self.waited[k] = val
        self.eo[waiter].wait_ge(sem, val)

    def _deps(self, waiter, r, w, wa=()):
        for k in wa:
            for ev in self.readers.get(k, {}).values():
                self._wait(waiter, ev)
        for k in r:
            for ev in self.lastw.get(k, ()):
                self._wait(waiter, ev)
        for k in w:
            for ev in self.lastw.get(k, ()):
                self._wait(waiter, ev)
            for ev in self.readers.get(k, {}).values():
                self._wait(waiter, ev)

    def _record(self, ev, src, r, w, wa=()):
        for k in wa:
            self.lastw.setdefault(k, []).append(ev)
        for k in w:
            self.lastw[k] = [ev]
            self.readers[k] = {}
        for k in r:
            self.readers.setdefault(k, {})[src] = ev

    def op(self, eng, fn, r=(), w=(), inc=True, wa=()):
        if self.plan:
            return None
        self._deps(eng, r, w, wa)
        ins = fn()
        sem, cnt = self.cur[eng]
        if inc:
            cnt += 1
            ins.then_inc(sem, 1)
            self.cur[eng][1] = cnt
            ev = (sem, cnt, eng)
            if cnt >= 30000:
                self.cur[eng] = [self.newsem(), 0]
        else:
            ev = (sem, cnt + 1, eng)
        self._record(ev, eng, r, w, wa)
        return ev

    def pe(self, fn, r=(), w=(), inc=True):
        return self.op('pe', fn, r, w, inc)

    def act(self, fn, r=(), w=()):
        return self.op('act', fn, r, w)

    def dve(self, fn, r=(), w=()):
        return self.op('dve', fn, r, w)

    def pool(self, fn, r=(), w=()):
        return self.op('pool', fn, r, w)

    def dma(self, q, out, in_, r=(), w=(), wa=(), **kw):
        if self.plan:
            return None
        if q not in self.dpool:
            self.dpool[q] = [[self.newsem(), 0] for _ in range(8)]
            self.dman[q] = 0
        n = self.dman[q]
        self.dman[q] += 1
        slot = self.dpool[q][n % 8]
        if slot[1] >= 30000:
            self.final.append((slot[0], slot[1]))
            slot[0], slot[1] = self.newsem(), 0
        sem, cnt = slot
        if cnt > 0:
            self._wait(q, (sem, cnt, 'dma'))
        self._deps(q, r, w, wa)
        self.eo[q].dma_start(out=out, in_=in_, **kw).then_inc(sem, 16)
        slot[1] = cnt + 16
        ev = (sem, cnt + 16, ('dma', q, n))
        self._record(ev, ('dma', q, n), r, w, wa)
        return ev

    def finish(self):
        for q, pool in self.dpool.items():
            for sem, cnt in pool:
                if cnt > 0:
                    self._wait('sp', (sem, cnt, 'dma'))
        for sem, cnt in self.final:
            self._wait('sp', (sem, cnt, 'dma'))


class WStream:
    def __init__(self, S, ring):
        self.S, self.ring = S, ring
        self.reqs = []
        self.i = 0
        self.issued = 0
        self.off = {}
        self.total = 0
        self.wscr = None

    CAP = 1000000

    def plan_total(self):
        self.assign = {}
        sizes = [0]
        for (wid, w_ap, k0, kcn, ranges) in self.reqs:
            key = (wid, k0, kcn, ranges)
            if key not in self.assign:
                n = kcn * sum(wd for _, wd in ranges)
                if sizes[-1] + n > self.CAP:
                    sizes.append(0)
                self.assign[key] = (len(sizes) - 1, sizes[-1])
                sizes[-1] += n
        return sizes

    def get(self, wid, w_ap, k0, kcn, ranges):
        if self.S.plan:
            self.reqs.append((wid, w_ap, k0, kcn, tuple(ranges)))
            return None, None
        i = self.i
        self.i += 1
        assert self.reqs[i][0] == wid and self.reqs[i][2:] == (k0, kcn, tuple(ranges))
        while self.issued < min(len(self.reqs), i + NSLOT):
            self._issue(self.issued)
            self.issued += 1
        slot = i % NSLOT
        W = sum(wd for _, wd in ranges)
        view = self.ring[:, slot, 0:kcn * W].rearrange("p (k n) -> p k n", n=W)
        return view, slot

    def _issue(self, j):
        wid, w_ap, k0, kcn, ranges = self.reqs[j]
        slot = j % NSLOT
        W = sum(wd for _, wd in ranges)
        n = kcn * W
        assert n <= SLABW
        key = (wid, k0, kcn, ranges)
        ti, o = self.assign[key]
        if key in self.off:
            self.S.dma('pool', out=self.ring[:, slot, 0:n], in_=self.wscr[ti][:, o:o + n], r=[('wscr', key)], w=[('slab', slot)])
            return
        view = self.ring[:, slot, 0:n].rearrange("p (k n) -> p k n", n=W)
        off = 0
        for ri, (c0, wd) in enumerate(ranges):
            self.S.dma('pool', out=view[:, :, off:off + wd],
                       in_=w_ap[k0 * 128:(k0 + kcn) * 128, c0:c0 + wd].rearrange("(k p) n -> p k n", p=128),
                       w=[('slab', slot)] if ri == 0 else [], wa=[('slab', slot)] if ri else [])
            off += wd
        self.off[key] = o
        self.S.dma('pool', out=self.wscr[ti][:, o:o + n], in_=self.ring[:, slot, 0:n], r=[('slab', slot)], w=[('wscr', key)])


def build_program():
    nc = bass.Bass("TRN2", target_bir_lowering=False)
    es = ExitStack()

    def din(name, shape):
        return nc.dram_tensor(name, list(shape), F32, kind="ExternalInput").ap()

    def dout(name, shape):
        return nc.dram_tensor("o_" + name, list(shape), F32, kind="ExternalOutput").ap()

    def dscr(name, shape, dt=BF16):
        return nc.dram_tensor(name, list(shape), dt, kind="Internal").ap()

    I = {}
    I['xp'] = din('xp', (SEQ, D))
    I['xs'] = din('xs', (DSEQ, D))
    for nm in ('cdk', 'cdv', 'cfk', 'cfv'):
        I[nm] = din(nm, (DEPTH, PAST, 1024))
    I['cdi'] = din('cdi', (DEPTH, PAST, 64))
    I['cfl'] = din('cfl', (DEPTH, PAST, 8))
    I['sca'] = din('sca', (DEPTH, 2, 1024))
    I['scd'] = din('scd', (DEPTH, 3, 1024))
    I['slru'] = din('slru', (DEPTH, 1024))
    I['ln_g'] = din('ln_g', (DEPTH, 3, D))
    I['ln_b'] = din('ln_b', (DEPTH, 3, D))
    I['w13'] = din('w13', (DEPTH, 2, D, 2 * DFF))
    I['w2'] = din('w2', (DEPTH, 2, DFF, D))
    I['w_in'] = din('w_in', (DEPTH, D, N_IN))
    I['fox_b_f'] = din('fox_b_f', (DEPTH, 8))
    I['conv_a_w'] = din('conv_a_w', (DEPTH, 3, 1024))
    I['conv_d_w'] = din('conv_d_w', (DEPTH, 4, 1024))
    I['conv_d_b'] = din('conv_d_b', (DEPTH, 1024))
    I['lru_wa'] = din('lru_wa', (DEPTH, 8, 128, 128))
    I['lru_ba'] = din('lru_ba', (DEPTH, 1024))
    I['lru_wx'] = din('lru_wx', (DEPTH, 8, 128, 128))
    I['lru_bx'] = din('lru_bx', (DEPTH, 1024))
    I['lru_lambda'] = din('lru_lambda', (DEPTH, 1024))
    I['w_out'] = din('w_out', (DEPTH, D, D))
    I['tq'] = din('tq', (SEQ, 128))
    I['ti'] = din('ti', (SEQ, 128))

    O = {}
    for pre, T in (('p', SEQ), ('s', DSEQ)):
        O[pre + 'y'] = dout(pre + 'y', (T, D))
        for nm in ('dk', 'dv', 'fk', 'fv'):
            O[pre + nm] = dout(pre + nm, (DEPTH, T, 1024))
        O[pre + 'di'] = dout(pre + 'di', (DEPTH, T, 64))
        O[pre + 'fl'] = dout(pre + 'fl', (DEPTH, T, 8))
        O[pre + 'ca'] = dout(pre + 'ca', (DEPTH, 2, 1024))
        O[pre + 'cd'] = dout(pre + 'cd', (DEPTH, 3, 1024))
        O[pre + 'lru'] = dout(pre + 'lru', (DEPTH, 1024))

    SC = {}
    for pre, T in (('p', SEQ), ('s', PAST + DSEQ)):
        for br in ('d', 'f'):
            SC[pre + 'kT' + br] = dscr(pre + 'kT' + br, (DEPTH, NH, 128, T))
            SC[pre + 'v' + br] = dscr(pre + 'v' + br, (DEPTH, T, 1024))

    with es:
        S = Sched(nc, es)

        def sb(name, shape, dt=F32):
            return es.enter_context(nc.sbuf_tensor(name, list(shape), dt))

        xres = sb('xres', (128, 2, D))
        xT = sb('xT', (128, 32, TT), BF16)
        hT = sb('hT', (128, 64, TT), BF16)
        ring = sb('ring', (128, NSLOT, SLABW), BF16)
        ident = sb('ident', (128, 128))
        ones_f = sb('ones_f', (128, 128))
        ones_b = sb('ones_b', (128, 128), BF16)
        tri_f = sb('tri_f', (128, 128))
        eps_t = sb('eps_t', (128, 1))
        gblk = sb('gblk', (128, 2, 512))
        bblk = sb('bblk', (128, 2, 512))
        lnst = sb('lnst', (128, 2, 8, 6))
        lnmv = sb('lnmv', (128, 2, 2))
        lnrs = sb('lnrs', (128, 2, 2))
        siltmp = sb('siltmp', (128, 2, TT))
        stage = sb('stage', (128, 2, 256))
        stgb = sb('stgb', (128, 2, 256), BF16)
        ropet = sb('ropet', (128, 4, 64))
        cstq = sb('cstq', (128, 2, 128))
        csti = sb('csti', (128, 2, 128))
        kiT = sb('kiT', (128, DEPTH, SEQ), BF16)
        ki2 = sb('ki2', (128, 128))
        wis = sb('wis', (128, 2, 16))
        qiT = sb('qiT', (128, 8, TT), BF16)
        score = sb('score', (128, SEQ))
        work = sb('work', (128, SEQ))
        m8 = sb('m8', (128, 8))
        rl = sb('rl', (128, 2, 512))
        maskT = sb('maskT', (128, 2, 16, 128), BF16)
        KTa = sb('KTa', (128, 2, SEQ), BF16)
        Va = sb('Va', (128, 16, 256), BF16)
        qTs = sb('qTs', (128, 2, TT), BF16)
        kTs = sb('kTs', (128, 2, TT), BF16)
        PT = sb('PT', (128, 2, 128), BF16)
        rden = sb('rden', (128, 2, 128))
        Fall = sb('Fall', (128, DEPTH, 16, 8))
        Fcar = sb('Fcar', (128, DEPTH, 8))
        Fref = sb('Fref', (128, 2, 8))
        fbias = sb('fbias', (128, 2, 8))
        lfs = sb('lfs', (128, 2, 8))
        bfb = sb('bfb', (128, DEPTH, 8))
        cva = sb('cva', (128, DEPTH, 8, 2))
        cvd = sb('cvd', (128, DEPTH, 8, 3))
        hl = sb('hl', (128, DEPTH, 8))
        caw = sb('caw', (128, DEPTH, 3, 8))
        cdw = sb('cdw', (128, DEPTH, 4, 8))
        cdb = sb('cdb', (128, DEPTH, 8))
        lba = sb('lba', (128, DEPTH, 8))
        lbx = sb('lbx', (128, DEPTH, 8))
        lcp = sb('lcp', (128, DEPTH, 8))
        lcp2 = sb('lcp2', (128, DEPTH, 8))
        lwa = sb('lwa', (128, 2, 128), BF16)
        lwx = sb('lwx', (128, 2, 128), BF16)
        ubuf = sb('ubuf', (128, TT + 4))
        ab1 = sb('ab1', (128, TT))
        ab2 = sb('ab2', (128, TT))
        ab3 = sb('ab3', (128, TT))
        ab4 = sb('ab4', (128, TT))
        ab5 = sb('ab5', (128, TT))
        xcb = sb('xcb', (128, TT), BF16)
        ps = es.enter_context(nc.psum_tensor('ps', [128, 8, 512], F32))

        ws = WStream(S, ring)
        cnt = {'aux': 0, 'dense': 0, 'gb': 0, 'sil': 0, 'stg': 0, 'pt': 0, 'rl': 0, 'lw': 0}

        def setup_consts():
            S.pool(lambda: nc.gpsimd.memset(ident[:], 0.0), w=['ident'])
            S.pool(lambda: nc.gpsimd.affine_select(out=ident[:], in_=ident[:], pattern=[[-1, 128]], base=0,
                                                   channel_multiplier=1, compare_op=ALU.not_equal, fill=1.0),
                   r=['ident'], w=['ident'])
            S.pool(lambda: nc.gpsimd.memset(ones_f[:], 1.0), w=['ones_f'])
            S.pool(lambda: nc.gpsimd.memset(ones_b[:], 1.0), w=['ones_b'])
            S.pool(lambda: nc.gpsimd.memset(eps_t[:], LN_EPS), w=['eps'])
            S.pool(lambda: nc.gpsimd.memset(tri_f[:], 1.0), w=['tri'])
            S.pool(lambda: nc.gpsimd.affine_select(out=tri_f[:], in_=tri_f[:], pattern=[[1, 128]], base=0,
                                                   channel_multiplier=-1, compare_op=ALU.is_ge, fill=0.0),
                   r=['tri'], w=['tri'])
            nc_ = nc
            with nc_.allow_non_contiguous_dma(reason="tiny per-channel parameter loads"):
                for l in range(DEPTH):
                    S.dma('sp', out=bfb[:, l, :], in_=I['fox_b_f'][l].partition_broadcast(128), wa=['bfb'])
                    for i in range(3):
                        S.dma('sp', out=caw[:, l, i, :], in_=I['conv_a_w'][l, i].rearrange("(j p) -> p j", p=128), wa=['caw'])
                    for i in range(4):
                        S.dma('sp', out=cdw[:, l, i, :], in_=I['conv_d_w'][l, i].rearrange("(j p) -> p j", p=128), wa=['cdw'])
                    S.dma('sp', out=cdb[:, l, :], in_=I['conv_d_b'][l].rearrange("(j p) -> p j", p=128), wa=['cdb'])
                    S.dma('sp', out=lba[:, l, :], in_=I['lru_ba'][l].rearrange("(j p) -> p j", p=128), wa=['lba'])
                    S.dma('sp', out=lbx[:, l, :], in_=I['lru_bx'][l].rearrange("(j p) -> p j", p=128), wa=['lbx'])
                    S.dma('sp', out=lcp[:, l, :], in_=I['lru_lambda'][l].rearrange("(j p) -> p j", p=128), wa=['lcp'])
            S.act(lambda: nc.scalar.activation(out=lcp[:], in_=lcp[:], func=AF.Exp, scale=-1.0), r=['lcp'], w=['lcp'])
            S.act(lambda: nc.scalar.activation(out=lcp[:], in_=lcp[:], func=AF.Ln, bias=1.0, scale=1.0), r=['lcp'], w=['lcp'])
            S.dve(lambda: nc.vector.tensor_scalar(out=lcp2[:], in0=lcp[:], scalar1=-16.0, scalar2=None, op0=ALU.mult),
                  r=['lcp'], w=['lcp2'])
            S.dve(lambda: nc.vector.tensor_scalar(out=lcp[:], in0=lcp[:], scalar1=-8.0, scalar2=None, op0=ALU.mult),
                  r=['lcp', 'lcp2'], w=['lcp'])

        def xkeys(chunks):
            return [('xT', ci) for ci in range(len(chunks))]

        def dense_tok(inT, in_keys, nkc, chunks, wid, w_ap, col0, ncols, consume, allbanks=False):
            base = (cnt['dense'] % (4 if allbanks else 2)) * 2
            cnt['dense'] += 1
            banks = [base + ci for ci in range(len(chunks))]
            for s0 in range(0, nkc, KCS):
                kcn = min(KCS, nkc - s0)
                slab, slot = ws.get(wid, w_ap, s0, kcn, [(col0, ncols)])
                for kk in range(kcn):
                    kc = s0 + kk
                    for ci, (c0, cn) in enumerate(chunks):
                        last = (kc == nkc - 1)
                        S.pe(lambda: nc.tensor.matmul(ps[:cn, banks[ci], :ncols], lhsT=inT[:, kc, c0:c0 + cn],
                                                      rhs=slab[:, kk, :ncols], start=(kc == 0), stop=last),
                             r=[('slab', slot)] + in_keys(kc, ci), w=[('ps', banks[ci])],
                             inc=(last or (kk == kcn - 1 and ci == len(chunks) - 1)))
            for ci, (c0, cn) in enumerate(chunks):
                consume(ci, c0, cn, banks[ci])

        def dense_feat(inT, in_keys, nkc, ntok, wid, w_ap, ranges, consume, allbanks=False):
            units = []
            for (c0, wd) in ranges:
                for u in range(wd // 128):
                    units.append((len(units)))
            nu = len(units)
            base = (cnt['dense'] % 2) * 4 if allbanks else 0
            cnt['dense'] += 1
            banks = [(base + u) % 8 for u in range(nu)]
            umap = []
            for ri, (c0, wd) in enumerate(ranges):
                for u in range(wd // 128):
                    umap.append(ri)
            for s0 in range(0, nkc, KCS):
                kcn = min(KCS, nkc - s0)
                slab, slot = ws.get(wid, w_ap, s0, kcn, ranges)
                for kk in range(kcn):
                    kc = s0 + kk
                    for u in range(nu):
                        last = (kc == nkc - 1)
                        S.pe(lambda: nc.tensor.matmul(ps[:, banks[u], :ntok], lhsT=slab[:, kk, u * 128:(u + 1) * 128],
                                                      rhs=inT[:, kc, :ntok], start=(kc == 0), stop=last),
                             r=[('slab', slot)] + in_keys(kc, None), w=[('ps', banks[u])],
                             inc=(last or (kk == kcn - 1 and u == nu - 1)))
            consume(banks)

        def xT_keys(kc, ci):
            if ci is None:
                return [('xT', 0), ('xT', 1)]
            return [('xT', ci)]

        def hT_keys(kc, ci):
            return [('hT', kc)]

        def ffn(l, i, chunks, ntok):
            w13 = I['w13'][l, i]
            w2 = I['w2'][l, i]
            for mg in range(DFF // 256):
                def cons(banks, mg=mg):
                    for u in range(2):
                        m = mg * 2 + u
                        sl = cnt['sil'] % 2
                        cnt['sil'] += 1
                        S.act(lambda: nc.scalar.activation(out=siltmp[:, sl, :ntok], in_=ps[:, banks[u], :ntok], func=AF.Silu),
                              r=[('ps', banks[u])], w=[('sil', sl)])
                        S.dve(lambda: nc.vector.tensor_tensor(out=hT[:, m, :ntok], in0=ps[:, banks[2 + u], :ntok],
                                                              in1=siltmp[:, sl, :ntok], op=ALU.mult),
                              r=[('ps', banks[2 + u]), ('sil', sl)], w=[('hT', m)])
                dense_feat(xT, xT_keys, 32, ntok, ('w13', l, i), w13, [(mg * 256, 256), (DFF + mg * 256, 256)], cons, allbanks=True)
            for dg in range(8):
                def cons2(ci, c0, cn, bank, dg=dg):
                    S.dve(lambda: nc.vector.scalar_tensor_tensor(out=xres[:cn, ci, dg * 512:(dg + 1) * 512], in0=ps[:cn, bank, :],
                                                                 scalar=0.5, in1=xres[:cn, ci, dg * 512:(dg + 1) * 512],
                                                                 op0=ALU.mult, op1=ALU.add),
                          r=[('ps', bank), ('x', ci, dg)], w=[('x', ci, dg)])
                dense_tok(hT, hT_keys, 64, chunks, ('w2', l, i), w2, dg * 512, 512, cons2, allbanks=True)

        def layer_norm(l, i, chunks, out_scale):
            for ci, (c0, cn) in enumerate(chunks):
                for blk in range(8):
                    S.dve(lambda: nc.vector.bn_stats(out=lnst[:cn, ci, blk, :], in_=xres[:cn, ci, blk * 512:(blk + 1) * 512]),
                          r=[('x', ci, blk)], w=[('lnst', ci, blk)])
                S.dve(lambda: nc.vector.bn_aggr(out=lnmv[:cn, ci, :], in_=lnst[:cn, ci, :, :].rearrange("p a b -> p (a b)")),
                      r=[('lnst', ci, b_) for b_ in range(8)], w=[('lnmv', ci)])
                S.act(lambda: nc.scalar.activation(out=lnrs[:cn, ci, 0:1], in_=lnmv[:cn, ci, 1:2], func=AF.Sqrt,
                                                   bias=eps_t[:cn, :], scale=1.0),
                      r=[('lnmv', ci), 'eps'], w=[('lnrs', ci)])
                S.dve(lambda: nc.vector.reciprocal(out=lnrs[:cn, ci, 0:1], in_=lnrs[:cn, ci, 0:1]), r=[('lnrs', ci)], w=[('lnrs', ci)])
                S.dve(lambda: nc.vector.tensor_scalar(out=lnrs[:cn, ci, 0:1], in0=lnrs[:cn, ci, 0:1], scalar1=float(out_scale),
                                                      scalar2=None, op0=ALU.mult), r=[('lnrs', ci)], w=[('lnrs', ci)])
                S.dve(lambda: nc.vector.tensor_scalar(out=lnrs[:cn, ci, 1:2], in0=lnmv[:cn, ci, 0:1], scalar1=-1.0,
                                                      scalar2=lnrs[:cn, ci, 0:1], op0=ALU.mult, op1=ALU.mult),
                      r=[('lnrs', ci), ('lnmv', ci)], w=[('lnrs', ci)])
            for blk in range(8):
                gs = cnt['gb'] % 2
                cnt['gb'] += 1
                S.dma('sp', out=gblk[:, gs, :], in_=I['ln_g'][l, i, blk * 512:(blk + 1) * 512].partition_broadcast(128), w=[('g', gs)])
                S.dma('sp', out=bblk[:, gs, :], in_=I['ln_b'][l, i, blk * 512:(blk + 1) * 512].partition_broadcast(128), w=[('b', gs)])
                for ci, (c0, cn) in enumerate(chunks):
                    xb = xres[:cn, ci, blk * 512:(blk + 1) * 512]
                    S.act(lambda: nc.scalar.activation(out=xb, in_=xb, func=AF.Identity, bias=lnrs[:cn, ci, 1:2],
                                                       scale=lnrs[:cn, ci, 0:1]),
                          r=[('x', ci, blk), ('lnrs', ci)], w=[('x', ci, blk)])
                    S.dve(lambda: nc.vector.tensor_tensor(out=xb, in0=xb, in1=gblk[:cn, gs, :], op=ALU.mult),
                          r=[('x', ci, blk), ('g', gs)], w=[('x', ci, blk)])
                    S.dve(lambda: nc.vector.scalar_tensor_tensor(out=xb, in0=bblk[:cn, gs, :], scalar=float(out_scale), in1=xb,
                                                                 op0=ALU.mult, op1=ALU.add),
                          r=[('x', ci, blk), ('b', gs)], w=[('x', ci, blk)])
            make_xT(chunks, 1.0 / out_scale)

        def make_xT(chunks, scale):
            for ci, (c0, cn) in enumerate(chunks):
                for g4 in range(8):
                    bank = 4 + (cnt['aux'] % 4)
                    cnt['aux'] += 1
                    for j in range(4):
                        c = g4 * 4 + j
                        S.pe(lambda: nc.tensor.transpose(ps[:, bank, j * 128:j * 128 + cn], xres[:cn, ci, c * 128:(c + 1) * 128],
                                                         ident[:cn, :cn]),
                             r=[('x', ci, c // 4), 'ident'], w=[('ps', bank)], inc=(j == 3))
                    src = ps[:, bank, :].rearrange("p (j t) -> p j t", t=128)[:, :, :cn]
                    dst = xT[:, g4 * 4:(g4 + 1) * 4, c0:c0 + cn]
                    if g4 % 2 == 0:
                        S.act(lambda: nc.scalar.activation(out=dst, in_=src, func=AF.Identity, scale=float(scale)),
                              r=[('ps', bank)], w=[('xT', ci)])
                    else:
                        S.dve(lambda: nc.vector.tensor_scalar(out=dst, in0=src, scalar1=float(scale), scalar2=None, op0=ALU.mult),
                              r=[('ps', bank)], w=[('xT', ci)])

        def tok_group(l, chunks, col0, ncols, consume):
            dense_tok(xT, xT_keys, 32, chunks, ('w_in', l), I['w_in'][l], col0, ncols, consume)

        def rope_inplace(st, cn, nh, hd, half, tab, ci, sk, tk):
            v = st.rearrange("p (h d) -> p h d", d=hd)
            x1 = v[:, :, 0:half]
            x2 = v[:, :, half:2 * half]
            n = nh * half
            cosv = tab[:cn, ci, 0:n].rearrange("p (h d) -> p h d", d=half)
            sinv = tab[:cn, ci, 64:64 + n].rearrange("p (h d) -> p h d", d=half)
            t = [ropet[:cn, k, 0:n].rearrange("p (h d) -> p h d", d=half) for k in range(4)]
            S.dve(lambda: nc.vector.tensor_tensor(out=t[0], in0=x1, in1=cosv, op=ALU.mult), r=[sk, tk], w=['ropet'])
            S.dve(lambda: nc.vector.tensor_tensor(out=t[1], in0=x2, in1=sinv, op=ALU.mult), r=[sk, tk], w=['ropet'])
            S.dve(lambda: nc.vector.tensor_tensor(out=t[2], in0=x2, in1=cosv, op=ALU.mult), r=[sk, tk], w=['ropet'])
            S.dve(lambda: nc.vector.tensor_tensor(out=t[3], in0=x1, in1=sinv, op=ALU.mult), r=[sk, tk], w=['ropet'])
            S.dve(lambda: nc.vector.tensor_tensor(out=x1, in0=t[0], in1=t[1], op=ALU.subtract), r=['ropet'], w=[sk])
            S.dve(lambda: nc.vector.tensor_tensor(out=x2, in0=t[2], in1=t[3], op=ALU.add), r=['ropet'], w=[sk])

        def transpose_to(dst_fn, src, cn, ncol128, key_r, key_w, dt_scale=1.0):
            for j0 in range(0, ncol128, 4):
                bank = 4 + (cnt['aux'] % 4)
                cnt['aux'] += 1
                nj = min(4, ncol128 - j0)
                for j in range(nj):
                    S.pe(lambda: nc.tensor.transpose(ps[:, bank, j * 128:j * 128 + cn], src[:, (j0 + j) * 128:(j0 + j + 1) * 128],
                                                     ident[:cn, :cn]),
                         r=key_r + ['ident'], w=[('ps', bank)], inc=(j == nj - 1))
                for j in range(nj):
                    d = dst_fn(j0 + j)
                    S.act(lambda: nc.scalar.activation(out=d, in_=ps[:, bank, j * 128:j * 128 + cn], func=AF.Identity, scale=1.0),
                          r=[('ps', bank)], w=key_w)

        def mixer(sq, l, t0, chunks, ntok):
            pre, past = sq['pre'], sq['past']
            pos0 = past + t0
            gch0 = pos0 // 128
            nq = len(chunks)
            for ci, (c0, cn) in enumerate(chunks):
                S.dma('sp', out=cstq[:cn, ci, :], in_=I['tq'][pos0 + c0:pos0 + c0 + cn, :], w=[('tabq', ci)])
                S.dma('sp', out=csti[:cn, ci, :], in_=I['ti'][pos0 + c0:pos0 + c0 + cn, :], w=[('tabi', ci)])

            def newstage():
                s_ = cnt['stg'] % 2
                cnt['stg'] += 1
                return s_

            def cons_ki(ci, c0, cn, bank):
                S.act(lambda: nc.scalar.activation(out=ki2[:cn, 0:64], in_=ps[:cn, bank, 0:64], func=AF.Identity, scale=1.0),
                      r=[('ps', bank)], w=['ki2'])
                S.act(lambda: nc.scalar.activation(out=wis[:cn, ci, :], in_=ps[:cn, bank, 64:80], func=AF.Identity, scale=1.0),
                      r=[('ps', bank)], w=[('wis', ci)])
                rope_inplace(ki2[:cn, 0:64], cn, 1, 64, 8, csti, ci, 'ki2', ('tabi', ci))
                S.dve(lambda: nc.vector.tensor_copy(out=ki2[:cn, 64:128], in_=ki2[:cn, 0:64]), r=['ki2'], w=['ki2'])
                S.dma('sp', out=O[pre + 'di'][l, t0 + c0:t0 + c0 + cn, :], in_=ki2[:cn, 0:64], r=['ki2'])
                transpose_to(lambda j: kiT[:, l, pos0 + c0:pos0 + c0 + cn], ki2[:cn, :], cn, 1, ['ki2'], [('kiT', l)])
            tok_group(l, chunks, O_BKI, 80, cons_ki)

            for g in range(2):
                def cons_qi(ci, c0, cn, bank, g=g):
                    dst = work[:cn, ci * 1024 + g * 512:ci * 1024 + (g + 1) * 512]
                    S.act(lambda: nc.scalar.activation(out=dst, in_=ps[:cn, bank, :], func=AF.Identity, scale=1.0),
                          r=[('ps', bank)], w=['work'])
                    rope_inplace(dst, cn, 8, 64, 8, csti, ci, 'work', ('tabi', ci))
                    transpose_to(lambda j: qiT[:, g * 4 + j, c0:c0 + cn], dst, cn, 4, ['work'], ['qiT'])
                tok_group(l, chunks, O_BQI + g * 512, 512, cons_qi)

            for qi_, (q0, qn) in enumerate(chunks):
                if pre == 'p':
                    s_end = pos0 + q0 + qn
                    gq = (pos0 + q0) // 128
                    use_topk = gq >= 2
                else:
                    s_end = past + DSEQ
                    use_topk = True
                for kb0 in range(0, s_end, 512):
                    w_ = min(512, s_end - kb0)
                    for h in range(16):
                        bank = 4 + (cnt['aux'] % 4)
                        cnt['aux'] += 1
                        pb = (h % 2) * 64
                        S.pe(lambda: nc.tensor.matmul(ps[:qn, bank, :w_], lhsT=qiT[pb:pb + 64, h // 2, q0:q0 + qn],
                                                      rhs=kiT[pb:pb + 64, l, kb0:kb0 + w_], start=True, stop=True),
                             r=['qiT', ('kiT', l)], w=[('ps', bank)])
                        rs_ = cnt['rl'] % 2
                        cnt['rl'] += 1
                        S.act(lambda: nc.scalar.activation(out=rl[:qn, rs_, :w_], in_=ps[:qn, bank, :w_], func=AF.Relu),
                              r=[('ps', bank)], w=[('rl', rs_)])
                        if h == 0:
                            S.dve(lambda: nc.vector.tensor_scalar(out=score[:qn, kb0:kb0 + w_], in0=rl[:qn, rs_, :w_],
                                                                  scalar1=wis[:qn, qi_, 0:1], scalar2=None, op0=ALU.mult),
                                  r=[('rl', rs_), ('wis', qi_)], w=['score'])
                        else:
                            S.dve(lambda: nc.vector.scalar_tensor_tensor(out=score[:qn, kb0:kb0 + w_], in0=rl[:qn, rs_, :w_],
                                                                         scalar=wis[:qn, qi_, h:h + 1], in1=score[:qn, kb0:kb0 + w_],
                                                                         op0=ALU.mult, op1=ALU.add),
                                  r=[('rl', rs_), ('wis', qi_), 'score'], w=['score'])
                if pre == 'p':
                    S.dve(lambda: nc.vector.memset(score[0:64, s_end - 64:s_end], NEG), r=['score'], w=['score'])
                if use_topk:
                    src = score
                    for rnd in range(TOPK // 8):
                        S.dve(lambda: nc.vector.max(out=m8[:qn, :], in_=src[:qn, 0:s_end]), r=['score', 'work'], w=['m8'])
                        if rnd < TOPK // 8 - 1:
                            S.dve(lambda: nc.vector.match_replace(out=work[:qn, 0:s_end], in_to_replace=m8[:qn, :],
                                                                  in_values=src[:qn, 0:s_end], imm_value=NEG),
                                  r=['score', 'work', 'm8'], w=['work'])
                            src = work
                    S.dve(lambda: nc.vector.tensor_scalar(out=work[:qn, 0:s_end], in0=score[:qn, 0:s_end], scalar1=m8[:qn, 7:8],
                                                          scalar2=None, op0=ALU.is_ge), r=['score', 'm8', 'work'], w=['work'])
                else:
                    S.dve(lambda: nc.vector.tensor_scalar(out=work[:qn, 0:s_end], in0=score[:qn, 0:s_end], scalar1=-1.0e29,
                                                          scalar2=None, op0=ALU.is_ge), r=['score', 'work'], w=['work'])
                nsc = (s_end + 127) // 128
                for sc in range(nsc):
                    sn = min(128, s_end - sc * 128)
                    bank = 4 + (cnt['aux'] % 4)
                    cnt['aux'] += 1
                    S.pe(lambda: nc.tensor.transpose(ps[:sn, bank, :qn], work[:qn, sc * 128:sc * 128 + sn], ident[:qn, :qn]),
                         r=['work', 'ident'], w=[('ps', bank)])
                    S.act(lambda: nc.scalar.activation(out=maskT[:sn, qi_, sc, :qn], in_=ps[:sn, bank, :qn], func=AF.Identity, scale=1.0),
                          r=[('ps', bank)], w=['maskT'])

            def cons_cf(ci, c0, cn, bank):
                S.dve(lambda: nc.vector.tensor_tensor(out=lfs[:cn, ci, :], in0=ps[:cn, bank, 0:8], in1=bfb[:cn, l, :], op=ALU.add),
                      r=[('ps', bank), 'bfb'], w=[('lfs', ci)])
                S.act(lambda: nc.scalar.activation(out=lfs[:cn, ci, :], in_=lfs[:cn, ci, :], func=AF.Exp, scale=-1.0),
                      r=[('lfs', ci)], w=[('lfs', ci)])
                S.act(lambda: nc.scalar.activation(out=lfs[:cn, ci, :], in_=lfs[:cn, ci, :], func=AF.Ln, bias=1.0, scale=1.0),
                      r=[('lfs', ci)], w=[('lfs', ci)])
                S.dve(lambda: nc.vector.tensor_scalar(out=lfs[:cn, ci, :], in0=lfs[:cn, ci, :], scalar1=-1.0, scalar2=None, op0=ALU.mult),
                      r=[('lfs', ci)], w=[('lfs', ci)])
                S.dma('sp', out=O[pre + 'fl'][l, t0 + c0:t0 + c0 + cn, :], in_=lfs[:cn, ci, :], r=[('lfs', ci)])
            tok_group(l, chunks, O_CF, 8, cons_cf)
            for ci, (c0, cn) in enumerate(chunks):
                cum_chunk(l, gch0 + ci, lfs[:cn, ci, :], cn, [('lfs', ci)])

            for br, oq, ok_, ov, ych in (('d', O_BQ, O_BK, O_BV, 8), ('f', O_CQ, O_CK, O_CV, 16)):
                for hg in range(4):
                    cs_ = slice(hg * 256, (hg + 1) * 256)

                    def cons_q(ci, c0, cn, bank):
                        s_ = newstage()
                        S.act(lambda: nc.scalar.activation(out=stage[:cn, s_, 0:256], in_=ps[:cn, bank, 0:256], func=AF.Identity, scale=1.0),
                              r=[('ps', bank)], w=[('stg', s_)])
                        if br == 'd':
                            rope_inplace(stage[:cn, s_, 0:256], cn, 2, 128, 16, cstq, ci, ('stg', s_), ('tabq', ci))
                        transpose_to(lambda j: qTs[:, j, c0:c0 + cn], stage[:cn, s_, 0:256], cn, 2, [('stg', s_)], ['qTs'])
                    tok_group(l, chunks, oq + hg * 256, 256, cons_q)

                    def cons_k(ci, c0, cn, bank):
                        s_ = newstage()
                        S.act(lambda: nc.scalar.activation(out=stage[:cn, s_, 0:256], in_=ps[:cn, bank, 0:256], func=AF.Identity, scale=1.0),
                              r=[('ps', bank)], w=[('stg', s_)])
                        if br == 'd':
                            rope_inplace(stage[:cn, s_, 0:256], cn, 2, 128, 16, cstq, ci, ('stg', s_), ('tabq', ci))
                        S.dma('sp', out=O[pre + br + 'k'][l, t0 + c0:t0 + c0 + cn, cs_], in_=stage[:cn, s_, 0:256], r=[('stg', s_)])
                        transpose_to(lambda j: kTs[:, j, c0:c0 + cn], stage[:cn, s_, 0:256], cn, 2, [('stg', s_)], ['kTs'])
                    tok_group(l, chunks, ok_ + hg * 256, 256, cons_k)
                    with nc.allow_non_contiguous_dma(reason="kT cache rows"):
                        S.dma('sp', out=SC[pre + 'kT' + br][l, hg * 2:(hg + 1) * 2, :, pos0:pos0 + ntok].rearrange("h d t -> d h t"),
                              in_=kTs[:, :, :ntok], r=['kTs'], wa=[('dkT', br, hg)])

                    def cons_v(ci, c0, cn, bank):
                        s_ = newstage()
                        S.act(lambda: nc.scalar.activation(out=stage[:cn, s_, 0:256], in_=ps[:cn, bank, 0:256], func=AF.Identity, scale=1.0),
                              r=[('ps', bank)], w=[('stg', s_)])
                        S.dma('sp', out=O[pre + br + 'v'][l, t0 + c0:t0 + c0 + cn, cs_], in_=stage[:cn, s_, 0:256], r=[('stg', s_)])
                        S.dve(lambda: nc.vector.tensor_copy(out=stgb[:cn, s_, 0:256], in_=stage[:cn, s_, 0:256]), r=[('stg', s_)], w=[('stgb', s_)])
                        S.dma('sp', out=SC[pre + 'v' + br][l, pos0 + c0:pos0 + c0 + cn, cs_], in_=stgb[:cn, s_, 0:256],
                              r=[('stgb', s_)], wa=[('dv', br, hg)])
                    tok_group(l, chunks, ov + hg * 256, 256, cons_v)

                    s_tot = pos0 + ntok
                    with nc.allow_non_contiguous_dma(reason="cache loads"):
                        S.dma('sp', out=KTa[:, :, 0:s_tot], in_=SC[pre + 'kT' + br][l, hg * 2:(hg + 1) * 2, :, 0:s_tot].rearrange("h d t -> d h t"),
                              r=[('dkT', br, hg)], w=['KTa'])
                        nfull = s_tot // 128
                        S.dma('sp', out=Va[:, 0:nfull, :],
                              in_=SC[pre + 'v' + br][l, 0:nfull * 128, cs_].rearrange("(c p) n -> p c n", p=128),
                              r=[('dv', br, hg)], w=['Va'])
                        if s_tot % 128:
                            rem = s_tot % 128
                            S.dma('sp', out=Va[:rem, nfull, :], in_=SC[pre + 'v' + br][l, nfull * 128:s_tot, cs_],
                                  r=[('dv', br, hg)], wa=['Va'])
                    for h4 in range(2):
                        h = hg * 2 + h4
                        for qi_, (q0, qn) in enumerate(chunks):
                            dsc = (gch0 + qi_) if pre == 'p' else (past // 128)
                            for sc in range(dsc + 1):
                                sn = min(128, s_tot - sc * 128)
                                pt_ = cnt['pt'] % 2
                                sb_ = 4 + pt_
                                cnt['pt'] += 1
                                S.pe(lambda: nc.tensor.matmul(ps[:sn, sb_, :qn], lhsT=KTa[:, h4, sc * 128:sc * 128 + sn],
                                                              rhs=qTs[:, h4, q0:q0 + qn], start=True, stop=True),
                                     r=['KTa', 'qTs'], w=[('ps', sb_)])
                                if br == 'f':
                                    S.dve(lambda: nc.vector.tensor_tensor(out=fbias[:sn, pt_, 0:1], in0=Fref[:sn, qi_, h:h + 1],
                                                                          in1=Fall[:sn, l, sc, h:h + 1], op=ALU.subtract),
                                          r=[('Fref', qi_), ('Fall', l)], w=[('fbias', pt_)])
                                    S.act(lambda: nc.scalar.activation(out=PT[:sn, pt_, :qn], in_=ps[:sn, sb_, :qn], func=AF.Exp,
                                                                       bias=fbias[:sn, pt_, 0:1], scale=HD ** -0.5),
                                          r=[('ps', sb_), ('fbias', pt_)], w=[('PT', pt_)])
                                    if sc == dsc:
                                        S.pool(lambda: nc.gpsimd.affine_select(out=PT[:sn, pt_, :qn], in_=PT[:sn, pt_, :qn],
                                                                               pattern=[[1, qn]], base=0, channel_multiplier=-1,
                                                                               compare_op=ALU.is_ge, fill=0.0),
                                               r=[('PT', pt_)], w=[('PT', pt_)])
                                else:
                                    S.act(lambda: nc.scalar.activation(out=PT[:sn, pt_, :qn], in_=ps[:sn, sb_, :qn], func=AF.Exp,
                                                                       scale=HD ** -0.5),
                                          r=[('ps', sb_)], w=[('PT', pt_)])
                                    S.dve(lambda: nc.vector.tensor_tensor(out=PT[:sn, pt_, :qn], in0=PT[:sn, pt_, :qn],
                                                                          in1=maskT[:sn, qi_, sc, :qn], op=ALU.mult),
                                          r=[('PT', pt_), 'maskT'], w=[('PT', pt_)])
                                S.pe(lambda: nc.tensor.matmul(ps[:, 6, :qn], lhsT=Va[:sn, sc, h4 * 128:(h4 + 1) * 128], rhs=PT[:sn, pt_, :qn],
                                                              start=(sc == 0), stop=(sc == dsc)),
                                     r=['Va', ('PT', pt_)], w=[('ps', 6)])
                                S.pe(lambda: nc.tensor.matmul(ps[:, 7, :qn], lhsT=ones_b[:sn, :], rhs=PT[:sn, pt_, :qn],
                                                              start=(sc == 0), stop=(sc == dsc)),
                                     r=['ones_b', ('PT', pt_)], w=[('ps', 7)])
                            rd = cnt['rl'] % 2
                            cnt['rl'] += 1
                            S.dve(lambda: nc.vector.reciprocal(out=rden[:, rd, :qn], in_=ps[:, 7, :qn]), r=[('ps', 7)], w=[('rden', rd)])
                            S.dve(lambda: nc.vector.tensor_tensor(out=hT[:, ych + h, q0:q0 + qn], in0=ps[:, 6, :qn], in1=rden[:, rd, :qn],
                                                                  op=ALU.mult),
                                  r=[('ps', 6), ('rden', rd)], w=[('hT', ych + h)])

            for j in range(8):
                def cons_a(banks, j=j):
                    bh, bb, bc = banks[0], banks[1], banks[2]
                    S.act(lambda: nc.scalar.activation(out=ab1[:, :ntok], in_=ps[:, bh, :ntok], func=AF.Identity, scale=1.0),
                          r=[('ps', bh)], w=['ab1'])
                    S.dve(lambda: nc.vector.tensor_copy(out=ubuf[:, 0:2], in_=cva[:, l, j, :]), r=[('cva', l)], w=['ubuf'])
                    S.dve(lambda: nc.vector.tensor_tensor(out=ubuf[:, 2:2 + ntok], in0=ps[:, bc, :ntok], in1=ab1[:, :ntok], op=ALU.mult),
                          r=[('ps', bc), 'ab1', 'ubuf'], w=['ubuf'])
                    S.dve(lambda: nc.vector.tensor_copy(out=cva[:, l, j, :], in_=ubuf[:, ntok:ntok + 2]), r=['ubuf'], w=[('cva', l)])
                    S.dve(lambda: nc.vector.tensor_scalar(out=ab2[:, :ntok], in0=ubuf[:, 0:ntok], scalar1=caw[:, l, 0, j:j + 1],
                                                          scalar2=None, op0=ALU.mult), r=['ubuf', 'caw'], w=['ab2'])
                    for i in (1, 2):
                        S.dve(lambda: nc.vector.scalar_tensor_tensor(out=ab2[:, :ntok], in0=ubuf[:, i:i + ntok], scalar=caw[:, l, i, j:j + 1],
                                                                     in1=ab2[:, :ntok], op0=ALU.mult, op1=ALU.add),
                              r=['ubuf', 'caw', 'ab2'], w=['ab2'])
                    S.dve(lambda: nc.vector.tensor_tensor(out=hT[:, j, :ntok], in0=ps[:, bb, :ntok], in1=ab2[:, :ntok], op=ALU.mult),
                          r=[('ps', bb), 'ab2'], w=[('hT', j)])
                dense_feat(xT, xT_keys, 32, ntok, ('w_in', l), I['w_in'][l],
                           [(O_AH + j * 128, 128), (O_AB + j * 128, 128), (O_AC + j * 128, 128)], cons_a)

            for j in range(8):
                lw = cnt['lw'] % 2
                cnt['lw'] += 1
                S.dma('pool', out=lwa[:, lw, :], in_=I['lru_wa'][l, j], w=[('lwa', lw)])
                S.dma('pool', out=lwx[:, lw, :], in_=I['lru_wx'][l, j], w=[('lwx', lw)])

                def cons_d(banks, j=j, lw=lw):
                    bx, bg = banks[0], banks[1]
                    S.dve(lambda: nc.vector.tensor_copy(out=ubuf[:, 0:3], in_=cvd[:, l, j, :]), r=[('cvd', l)], w=['ubuf'])
                    S.act(lambda: nc.scalar.activation(out=ubuf[:, 3:3 + ntok], in_=ps[:, bx, :ntok], func=AF.Identity, scale=1.0),
                          r=[('ps', bx), 'ubuf'], w=['ubuf'])
                    S.dve(lambda: nc.vector.tensor_copy(out=cvd[:, l, j, :], in_=ubuf[:, ntok:ntok + 3]), r=['ubuf'], w=[('cvd', l)])
                    S.dve(lambda: nc.vector.tensor_scalar(out=ab1[:, :ntok], in0=ubuf[:, 0:ntok], scalar1=cdw[:, l, 0, j:j + 1],
                                                          scalar2=cdb[:, l, j:j + 1], op0=ALU.mult, op1=ALU.add),
                          r=['ubuf', 'cdw', 'cdb'], w=['ab1'])
                    for i in (1, 2, 3):
                        S.dve(lambda: nc.vector.scalar_tensor_tensor(out=ab1[:, :ntok], in0=ubuf[:, i:i + ntok], scalar=cdw[:, l, i, j:j + 1],
                                                                     in1=ab1[:, :ntok], op0=ALU.mult, op1=ALU.add),
                              r=['ubuf', 'cdw', 'ab1'], w=['ab1'])
                    S.dve(lambda: nc.vector.tensor_copy(out=xcb[:, :ntok], in_=ab1[:, :ntok]), r=['ab1'], w=['xcb'])
                    S.act(lambda: nc.scalar.activation(out=ab5[:, :ntok], in_=ps[:, bg, :ntok], func=AF.Identity, scale=1.0),
                          r=[('ps', bg)], w=['ab5'])
                    S.dve(lambda: nc.vector.tensor_tensor(out=ab4[:, :ntok], in0=ab5[:, :ntok], in1=ab5[:, :ntok], op=ALU.mult),
                          r=['ab5'], w=['ab4'])
                    S.dve(lambda: nc.vector.tensor_scalar(out=ab4[:, :ntok], in0=ab4[:, :ntok], scalar1=0.044715 * 1.5957691216,
                                                          scalar2=1.5957691216, op0=ALU.mult, op1=ALU.add), r=['ab4'], w=['ab4'])
                    S.dve(lambda: nc.vector.tensor_tensor(out=ab4[:, :ntok], in0=ab4[:, :ntok], in1=ab5[:, :ntok], op=ALU.mult),
                          r=['ab4', 'ab5'], w=['ab4'])
                    S.act(lambda: nc.scalar.activation(out=ab4[:, :ntok], in_=ab4[:, :ntok], func=AF.Sigmoid), r=['ab4'], w=['ab4'])
                    S.dve(lambda: nc.vector.tensor_tensor(out=ab5[:, :ntok], in0=ab4[:, :ntok], in1=ab5[:, :ntok], op=ALU.mult),
                          r=['ab4', 'ab5'], w=['ab5'])
                    S.pe(lambda: nc.tensor.matmul(ps[:, bx, :ntok], lhsT=lwa[:, lw, :], rhs=xcb[:, :ntok], start=True, stop=True),
                         r=[('lwa', lw), 'xcb'], w=[('ps', bx)])
                    S.pe(lambda: nc.tensor.matmul(ps[:, bg, :ntok], lhsT=lwx[:, lw, :], rhs=xcb[:, :ntok], start=True, stop=True),
                         r=[('lwx', lw), 'xcb'], w=[('ps', bg)])
                    S.act(lambda: nc.scalar.activation(out=ab2[:, :ntok], in_=ps[:, bx, :ntok], func=AF.Sigmoid, bias=lba[:, l, j:j + 1], scale=1.0),
                          r=[('ps', bx), 'lba'], w=['ab2'])
                    S.act(lambda: nc.scalar.activation(out=ab3[:, :ntok], in_=ps[:, bg, :ntok], func=AF.Sigmoid, bias=lbx[:, l, j:j + 1], scale=1.0),
                          r=[('ps', bg), 'lbx'], w=['ab3'])
                    S.act(lambda: nc.scalar.activation(out=ab4[:, :ntok], in_=ab2[:, :ntok], func=AF.Exp, scale=lcp2[:, l, j:j + 1]),
                          r=['ab2', 'lcp2'], w=['ab4'])
                    S.act(lambda: nc.scalar.activation(out=ab2[:, :ntok], in_=ab2[:, :ntok], func=AF.Exp, scale=lcp[:, l, j:j + 1]),
                          r=['ab2', 'lcp'], w=['ab2'])
                    S.dve(lambda: nc.vector.tensor_scalar(out=ab4[:, :ntok], in0=ab4[:, :ntok], scalar1=-1.0, scalar2=1.0,
                                                          op0=ALU.mult, op1=ALU.add), r=['ab4'], w=['ab4'])
                    S.dve(lambda: nc.vector.tensor_scalar(out=ab4[:, :ntok], in0=ab4[:, :ntok], scalar1=0.0, scalar2=None, op0=ALU.max),
                          r=['ab4'], w=['ab4'])
                    S.act(lambda: nc.scalar.activation(out=ab4[:, :ntok], in_=ab4[:, :ntok], func=AF.Sqrt), r=['ab4'], w=['ab4'])
                    S.dve(lambda: nc.vector.tensor_tensor(out=ab4[:, :ntok], in0=ab4[:, :ntok], in1=ab3[:, :ntok], op=ALU.mult),
                          r=['ab4', 'ab3'], w=['ab4'])
                    S.dve(lambda: nc.vector.tensor_tensor(out=ab4[:, :ntok], in0=ab4[:, :ntok], in1=ab1[:, :ntok], op=ALU.mult),
                          r=['ab4', 'ab1'], w=['ab4'])
                    S.dve(lambda: nc.vector.tensor_tensor_scan(out=ab3[:, :ntok], data0=ab2[:, :ntok], data1=ab4[:, :ntok],
                                                               initial=hl[:, l, j:j + 1], op0=ALU.mult, op1=ALU.add),
                          r=['ab2', 'ab4', ('hl', l)], w=['ab3'])
                    S.dve(lambda: nc.vector.tensor_copy(out=hl[:, l, j:j + 1], in_=ab3[:, ntok - 1:ntok]), r=['ab3'], w=[('hl', l)])
                    S.dve(lambda: nc.vector.tensor_tensor(out=hT[:, 24 + j, :ntok], in0=ab3[:, :ntok], in1=ab5[:, :ntok], op=ALU.mult),
                          r=['ab3', 'ab5'], w=[('hT', 24 + j)])
                dense_feat(xT, xT_keys, 32, ntok, ('w_in', l), I['w_in'][l], [(O_DX + j * 128, 128), (O_DG + j * 128, 128)], cons_d)

        def cum_chunk(l, gc, lf_ap, cn, keys):
            bank = 4 + (cnt['aux'] % 4)
            cnt['aux'] += 1
            S.pe(lambda: nc.tensor.matmul(ps[:cn, bank, 0:8], lhsT=tri_f[:cn, :cn], rhs=lf_ap, start=True, stop=True),
                 r=keys + ['tri'], w=[('ps', bank)])
            S.dve(lambda: nc.vector.tensor_tensor(out=Fall[:cn, l, gc, :], in0=ps[:cn, bank, 0:8], in1=Fcar[:cn, l, :], op=ALU.add),
                  r=[('ps', bank), ('Fcar', l)], w=[('Fall', l)])
            bank2 = 4 + (cnt['aux'] % 4)
            cnt['aux'] += 1
            S.pe(lambda: nc.tensor.matmul(ps[:, bank2, 0:8], lhsT=ones_f[:cn, :], rhs=lf_ap, start=True, stop=True),
                 r=keys + ['ones_f'], w=[('ps', bank2)])
            qi_ = gc % 2 if cn == 128 else 0
            S.dve(lambda: nc.vector.tensor_copy(out=Fref[:, qi_, :], in_=Fcar[:, l, :]), r=[('Fcar', l)], w=[('Fref', qi_)])
            S.dve(lambda: nc.vector.tensor_tensor(out=Fcar[:, l, :], in0=ps[:, bank2, 0:8], in1=Fcar[:, l, :], op=ALU.add),
                  r=[('ps', bank2), ('Fcar', l), ('Fref', qi_)], w=[('Fcar', l)])

        def init_seq(sq):
            pre, past = sq['pre'], sq['past']
            for l in range(DEPTH):
                S.dve(lambda: nc.vector.memset(Fcar[:, l, :], 0.0), r=[('Fcar', l)], w=[('Fcar', l)])
                if pre == 'p':
                    S.dve(lambda: nc.vector.memset(cva[:, l, :, :], 0.0), r=[('cva', l)], w=[('cva', l)])
                    S.dve(lambda: nc.vector.memset(cvd[:, l, :, :], 0.0), r=[('cvd', l)], w=[('cvd', l)])
                    S.dve(lambda: nc.vector.memset(hl[:, l, :], 0.0), r=[('hl', l)], w=[('hl', l)])
                else:
                    with nc.allow_non_contiguous_dma(reason="tiny state loads"):
                        for i in range(2):
                            S.dma('sp', out=cva[:, l, :, i], in_=I['sca'][l, i].rearrange("(j p) -> p j", p=128), wa=[('cva', l)])
                        for i in range(3):
                            S.dma('sp', out=cvd[:, l, :, i], in_=I['scd'][l, i].rearrange("(j p) -> p j", p=128), wa=[('cvd', l)])
                        S.dma('sp', out=hl[:, l, :], in_=I['slru'][l].rearrange("(j p) -> p j", p=128), wa=[('hl', l)])
                    for br, kn, vn in (('d', 'cdk', 'cdv'), ('f', 'cfk', 'cfv')):
                        for hg in range(4):
                            cs_ = slice(hg * 256, (hg + 1) * 256)
                            S.dma('pool', out=SC['sv' + br][l, 0:PAST, cs_], in_=I[vn][l, :, cs_], wa=[('dv', br, hg)])
                            for c in range(PAST // 128):
                                s_ = cnt['stg'] % 2
                                cnt['stg'] += 1
                                S.dma('sp', out=stage[:, s_, 0:256], in_=I[kn][l, c * 128:(c + 1) * 128, cs_], w=[('stg', s_)])
                                transpose_to(lambda j: kTs[:, j, 0:128], stage[:, s_, 0:256], 128, 2, [('stg', s_)], ['kTs'])
                                with nc.allow_non_contiguous_dma(reason="kT cache rows"):
                                    S.dma('sp', out=SC['skT' + br][l, hg * 2:(hg + 1) * 2, :, c * 128:(c + 1) * 128].rearrange("h d t -> d h t"),
                                          in_=kTs[:, :, 0:128], r=['kTs'], wa=[('dkT', br, hg)])
                    for c in range(PAST // 128):
                        S.dma('sp', out=ki2[:, 0:64], in_=I['cdi'][l, c * 128:(c + 1) * 128, :], w=['ki2'])
                        S.dve(lambda: nc.vector.tensor_copy(out=ki2[:, 64:128], in_=ki2[:, 0:64]), r=['ki2'], w=['ki2'])
                        transpose_to(lambda j: kiT[:, l, c * 128:(c + 1) * 128], ki2[:, :], 128, 1, ['ki2'], [('kiT', l)])
                        S.dma('sp', out=lfs[:, 0, :], in_=I['cfl'][l, c * 128:(c + 1) * 128, :], w=[('lfs', 0)])
                        cum_chunk(l, c, lfs[:, 0, :], 128, [('lfs', 0)])

        def final_states(sq):
            pre = sq['pre']
            with nc.allow_non_contiguous_dma(reason="tiny state stores"):
                for l in range(DEPTH):
                    for i in range(2):
                        S.dma('sp', out=O[pre + 'ca'][l, i].rearrange("(j p) -> p j", p=128), in_=cva[:, l, :, i], r=[('cva', l)])
                    for i in range(3):
                        S.dma('sp', out=O[pre + 'cd'][l, i].rearrange("(j p) -> p j", p=128), in_=cvd[:, l, :, i], r=[('cvd', l)])
                    S.dma('sp', out=O[pre + 'lru'][l].rearrange("(j p) -> p j", p=128), in_=hl[:, l, :], r=[('hl', l)])

        def emit():
            cnt.update({k: 0 for k in cnt})
            setup_consts()
            seqs = [dict(pre='p', T=SEQ, past=0, x=I['xp'], y=O['py']),
                    dict(pre='s', T=DSEQ, past=PAST, x=I['xs'], y=O['sy'])]
            for sq in seqs:
                init_seq(sq)
                T = sq['T']
                for t0 in range(0, T, TT):
                    ntok = min(TT, T - t0)
                    chunks = [(c0, min(128, ntok - c0)) for c0 in range(0, ntok, 128)]
                    for ci, (c0, cn) in enumerate(chunks):
                        S.dma('sp', out=xres[:cn, ci, :], in_=sq['x'][t0 + c0:t0 + c0 + cn, :], w=[('x', ci, b_) for b_ in range(8)])
                    make_xT(chunks, 1.0)
                    for ci, (c0, cn) in enumerate(chunks):
                        for blk in range(8):
                            S.pool(lambda: nc.gpsimd.tensor_scalar(out=xres[:cn, ci, blk * 512:(blk + 1) * 512],
                                                                   in0=xres[:cn, ci, blk * 512:(blk + 1) * 512], scalar1=ALPHA, scalar2=None,
                                                                   op0=ALU.mult), r=[('x', ci, blk)], w=[('x', ci, blk)])
                    for l in range(DEPTH):
                        ffn(l, 0, chunks, ntok)
                        layer_norm(l, 0, chunks, ALPHA)
                        mixer(sq, l, t0, chunks, ntok)

                        for dg in range(8):
                            def cons_o(ci, c0, cn, bank, dg=dg):
                                S.dve(lambda: nc.vector.tensor_tensor(out=xres[:cn, ci, dg * 512:(dg + 1) * 512], in0=ps[:cn, bank, :],
                                                                      in1=xres[:cn, ci, dg * 512:(dg + 1) * 512], op=ALU.add),
                                      r=[('ps', bank), ('x', ci, dg)], w=[('x', ci, dg)])
                            dense_tok(hT, hT_keys, 32, chunks, ('w_out', l), I['w_out'][l], dg * 512, 512, cons_o, allbanks=True)
                        layer_norm(l, 1, chunks, ALPHA)
                        ffn(l, 1, chunks, ntok)
                        layer_norm(l, 2, chunks, ALPHA if l < DEPTH - 1 else 1.0)
                    for ci, (c0, cn) in enumerate(chunks):
                        S.dma('sp', out=sq['y'][t0 + c0:t0 + c0 + cn, :], in_=xres[:cn, ci, :], r=[('x', ci, b_) for b_ in range(8)])
                final_states(sq)
            S.finish()

        S.plan = True
        emit()
        S.plan = False
        ws.wscr = [dscr('wscr%d' % ti_, (128, sz_)) for ti_, sz_ in enumerate(ws.plan_total())]
        emit()
    return nc


_ROPE_THETA = 500000.0


def _rope_tables():
    pos = np.arange(SEQ, dtype=np.float32)
    inv16 = (_ROPE_THETA ** (-np.arange(16, dtype=np.float32) / 16)).astype(np.float32)
    inv8 = (_ROPE_THETA ** (-np.arange(8, dtype=np.float32) / 8)).astype(np.float32)
    a16 = pos[:, None] * inv16[None, :]
    a8 = pos[:, None] * inv8[None, :]
    tq = np.concatenate([np.tile(np.cos(a16), (1, 4)), np.tile(np.sin(a16), (1, 4))], axis=1).astype(np.float32)
    ti = np.concatenate([np.tile(np.cos(a8), (1, 8)), np.tile(np.sin(a8), (1, 8))], axis=1).astype(np.float32)
    return np.ascontiguousarray(tq), np.ascontiguousarray(ti)


def kernel(x_prompt, x_sample, cache_dsa_k, cache_dsa_v, cache_dsa_kidx, cache_fox_k, cache_fox_v,
           cache_fox_logf, state_conv_a, state_conv_d, state_lru, ln_g, ln_b, ffn_w13, ffn_w2, w_in,
           fox_b_f, conv_a_w, conv_d_w, conv_d_b, lru_wa, lru_ba, lru_wx, lru_bx, lru_lambda, w_out):
    f = lambda a: np.ascontiguousarray(np.asarray(a, dtype=np.float32))
    nc = build_program()
    tq, ti = _rope_tables()
    shared = dict(ln_g=f(ln_g), ln_b=f(ln_b), w13=f(ffn_w13), w2=f(ffn_w2), w_in=f(w_in), fox_b_f=f(fox_b_f),
                  conv_a_w=f(conv_a_w), conv_d_w=f(conv_d_w), conv_d_b=f(conv_d_b), lru_wa=f(lru_wa), lru_ba=f(lru_ba),
                  lru_wx=f(lru_wx), lru_bx=f(lru_bx), lru_lambda=f(lru_lambda), w_out=f(w_out), tq=tq, ti=ti)
    in_maps = []
    for b in range(8):
        m = dict(shared)
        m['xp'] = f(x_prompt[b])
        m['xs'] = f(x_sample[b])
        m['cdk'] = f(np.asarray(cache_dsa_k)[:, b].reshape(DEPTH, PAST, 1024))
        m['cdv'] = f(np.asarray(cache_dsa_v)[:, b].reshape(DEPTH, PAST, 1024))
        m['cfk'] = f(np.asarray(cache_fox_k)[:, b].reshape(DEPTH, PAST, 1024))
        m['cfv'] = f(np.asarray(cache_fox_v)[:, b].reshape(DEPTH, PAST, 1024))
        m['cdi'] = f(np.asarray(cache_dsa_kidx)[:, b])
        m['cfl'] = f(np.asarray(cache_fox_logf)[:, b])
        m['sca'] = f(np.asarray(state_conv_a)[:, b])
        m['scd'] = f(np.asarray(state_conv_d)[:, b])
        m['slru'] = f(np.asarray(state_lru)[:, b])
        in_maps.append(m)
    res = run_bass_kernel_spmd(nc, in_maps, core_ids=list(range(8)))
    R = res.results

    def gather(name, shape_tail):
        return np.stack([np.asarray(R[b]['o_' + name], dtype=np.float32) for b in range(8)], axis=0)

    outs = [gather('py', None), gather('sy', None)]
    for pre, T in (('p', SEQ), ('s', DSEQ)):
        def st(name, tail):
            a = gather(pre + name, None)
            a = np.moveaxis(a, 0, 1)
            return np.ascontiguousarray(a.reshape((DEPTH, 8) + tail))
        outs += [st('dk', (T, NH, HD)), st('dv', (T, NH, HD)), st('di', (T, 64)), st('fk', (T, NH, HD)), st('fv', (T, NH, HD)),
                 st('fl', (T, NH)), st('ca', (2, 1024)), st('cd', (3, 1024)), st('lru', (1024,))]
    return tuple(outs)
```
